# Optimizing a Trainium2 kernel written in Bass

```python
import math
import jax, jax.numpy as jnp
from jax import lax
import numpy as np

D_MODEL = 4096
BATCH = 2
SEQ = 4096
DEPTH = 4

N_MIXERS = 2
ROPE_THETA = 500000.0
EPS = 1e-6
Q_BLOCK = 128
D_FF = 4 * D_MODEL
PLE_DIM = 256

DIFF_HEAD_DIM = 128
DIFF_HEADS = D_MODEL // (2 * DIFF_HEAD_DIM)
DIFF_ROT = DIFF_HEAD_DIM // 4
DIFF_QK_WIDTH = DIFF_HEADS * 2 * DIFF_HEAD_DIM
DIFF_V_WIDTH = DIFF_HEADS * 2 * DIFF_HEAD_DIM
DIFF_IN_WIDTH = 2 * DIFF_QK_WIDTH + DIFF_V_WIDTH

MLA_HEADS = 32
MLA_NOPE = 128
MLA_ROPE = 64
MLA_V = 128
MLA_QK = MLA_NOPE + MLA_ROPE
MLA_Q_RANK = 1024
MLA_KV_RANK = 512
MLA_IN_WIDTH = MLA_Q_RANK + MLA_KV_RANK + MLA_ROPE

N_DIFF_LAYERS = (DEPTH + 1) // 2
N_MLA_LAYERS = DEPTH // 2

kernel_name = "hybrid_diffattn_mla_encoder"


def rms_norm(x, g):
    x32 = x.astype(jnp.float32)
    y = x32 * lax.rsqrt(jnp.mean(x32 * x32, axis=-1, keepdims=True) + EPS)
    return (y * g.astype(jnp.float32)).astype(x.dtype)


def rope_tables(positions, rot_dim):
    inv = ROPE_THETA ** (-jnp.arange(0, rot_dim, 2, dtype=jnp.float32) / rot_dim)
    ang = positions.astype(jnp.float32)[..., None] * inv
    return jnp.cos(ang), jnp.sin(ang)


def apply_rope(x, cos, sin):
    x32 = x.astype(jnp.float32)
    x1, x2 = jnp.split(x32, 2, axis=-1)
    out = jnp.concatenate([x1 * cos - x2 * sin, x2 * cos + x1 * sin], axis=-1)
    return out.astype(x.dtype)


def to_blocks(t):
    b, s = t.shape[0], t.shape[1]
    t = t.reshape((b, s // Q_BLOCK, Q_BLOCK) + t.shape[2:])
    return jnp.moveaxis(t, 1, 0)


def from_blocks(t):
    t = jnp.moveaxis(t, 0, 1)
    return t.reshape((t.shape[0], t.shape[1] * t.shape[2]) + t.shape[3:])


def diff_attention(h, w_in, w_out, g_q, g_k, lam, g_sub, cos, sin, lambda_init):
    b, s, _ = h.shape
    z = h @ w_in
    q, k, v = jnp.split(z, [DIFF_QK_WIDTH, 2 * DIFF_QK_WIDTH], axis=-1)
    q = q.reshape(b, s, DIFF_HEADS, 2, DIFF_HEAD_DIM)
    k = k.reshape(b, s, DIFF_HEADS, 2, DIFF_HEAD_DIM)
    v = v.reshape(b, s, DIFF_HEADS, 2 * DIFF_HEAD_DIM)
    c5 = cos[:, :, None, None, :]
    s5 = sin[:, :, None, None, :]
    q = rms_norm(q, g_q)
    k = rms_norm(k, g_k)
    q = jnp.concatenate([apply_rope(q[..., :DIFF_ROT], c5, s5), q[..., DIFF_ROT:]], axis=-1)
    k = jnp.concatenate([apply_rope(k[..., :DIFF_ROT], c5, s5), k[..., DIFF_ROT:]], axis=-1)
    lam32 = lam.astype(jnp.float32)
    lam_full = (jnp.exp(jnp.sum(lam32[0] * lam32[1]))
                - jnp.exp(jnp.sum(lam32[2] * lam32[3])) + lambda_init)
    scale = DIFF_HEAD_DIM ** -0.5

    def block(qb):
        sc = jnp.einsum('bqhcd,bkhcd->bhcqk', qb, k).astype(jnp.float32) * scale
        pr = jax.nn.softmax(sc, axis=-1)
        a = pr[:, :, 0] - lam_full * pr[:, :, 1]
        return jnp.einsum('bhqk,bkhe->bqhe', a.astype(v.dtype), v)

    o = from_blocks(lax.map(block, to_blocks(q)))
    o = rms_norm(o, g_sub) * (1.0 - lambda_init)
    return o.reshape(b, s, DIFF_V_WIDTH) @ w_out


def mla_attention(h, w_in, g_cq, g_ckv, w_uq, w_ukv, g_q, g_k, w_out, cos, sin):
    b, s, _ = h.shape
    z = h @ w_in
    c_q, c_kv, k_rope = jnp.split(z, [MLA_Q_RANK, MLA_Q_RANK + MLA_KV_RANK], axis=-1)
    q = (rms_norm(c_q, g_cq) @ w_uq).reshape(b, s, MLA_HEADS, MLA_QK)
    kv = (rms_norm(c_kv, g_ckv) @ w_ukv).reshape(b, s, MLA_HEADS, MLA_NOPE + MLA_V)
    k_nope, v = jnp.split(kv, [MLA_NOPE], axis=-1)
    q_nope = rms_norm(q[..., :MLA_NOPE], g_q[:MLA_NOPE])
    q_rope = rms_norm(q[..., MLA_NOPE:], g_q[MLA_NOPE:])
    k_nope = rms_norm(k_nope, g_k[:MLA_NOPE])
    k_rope = rms_norm(k_rope, g_k[MLA_NOPE:])
    q_rope = apply_rope(q_rope, cos[:, :, None, :], sin[:, :, None, :])
    k_rope = apply_rope(k_rope, cos, sin)
    q = jnp.concatenate([q_nope, q_rope], axis=-1)
    k = jnp.concatenate(
        [k_nope, jnp.broadcast_to(k_rope[:, :, None, :], (b, s, MLA_HEADS, MLA_ROPE))], axis=-1)
    scale = MLA_QK ** -0.5

    def block(qb):
        sc = jnp.einsum('bqhd,bkhd->bhqk', qb, k).astype(jnp.float32) * scale
        pr = jax.nn.softmax(sc, axis=-1).astype(v.dtype)
        return jnp.einsum('bhqk,bkhe->bqhe', pr, v)

    o = from_blocks(lax.map(block, to_blocks(q)))
    return o.reshape(b, s, MLA_HEADS * MLA_V) @ w_out


def sq_relu_mlp(h, w1, w2):
    a = jax.nn.relu(h @ w1)
    return (a * a) @ w2


def setup_inputs(seed: int = 0) -> dict:
    key = jax.random.key(seed)
    ks = jax.random.split(key, 24)
    f32 = jnp.float32

    def nrm(k, shape, fan_in):
        return jax.random.normal(k, shape, f32) * (fan_in ** -0.5)

    def gain(k, shape):
        return 1.0 + 0.01 * jax.random.normal(k, shape, f32)

    ND, NM = N_DIFF_LAYERS, N_MLA_LAYERS
    x = jax.random.normal(ks[0], (BATCH, SEQ, D_MODEL), f32)
    p = jax.random.normal(ks[1], (DEPTH, BATCH, SEQ, PLE_DIM), f32)
    positions = jnp.broadcast_to(jnp.arange(SEQ, dtype=jnp.int32), (BATCH, SEQ))
    return {
        "x": x,
        "p": p,
        "positions": positions,
        "g_mix": gain(ks[2], (DEPTH, D_MODEL)),
        "g_mlp": gain(ks[3], (DEPTH, D_MODEL)),
        "g_ple": gain(ks[4], (DEPTH, D_MODEL)),
        "w1": nrm(ks[5], (DEPTH, D_MODEL, D_FF), D_MODEL),
        "w2": nrm(ks[6], (DEPTH, D_FF, D_MODEL), D_FF),
        "w_gate": nrm(ks[7], (DEPTH, D_MODEL, D_MODEL), D_MODEL),
        "w_ple": nrm(ks[8], (DEPTH, PLE_DIM, D_MODEL), PLE_DIM),
        "diff_w_in": nrm(ks[9], (ND, D_MODEL, DIFF_IN_WIDTH), D_MODEL),
        "diff_w_out": nrm(ks[10], (ND, DIFF_V_WIDTH, D_MODEL), DIFF_V_WIDTH),
        "diff_g_q": gain(ks[11], (ND, DIFF_HEAD_DIM)),
        "diff_g_k": gain(ks[12], (ND, DIFF_HEAD_DIM)),
        "diff_lambda": 0.1 * jax.random.normal(ks[13], (ND, 4, DIFF_HEAD_DIM), f32),
        "diff_g_sub": gain(ks[14], (ND, 2 * DIFF_HEAD_DIM)),
        "mla_w_in": nrm(ks[15], (NM, D_MODEL, MLA_IN_WIDTH), D_MODEL),
        "mla_g_cq": gain(ks[16], (NM, MLA_Q_RANK)),
        "mla_g_ckv": gain(ks[17], (NM, MLA_KV_RANK)),
        "mla_w_uq": nrm(ks[18], (NM, MLA_Q_RANK, MLA_HEADS * MLA_QK), MLA_Q_RANK),
        "mla_w_ukv": nrm(ks[19], (NM, MLA_KV_RANK, MLA_HEADS * (MLA_NOPE + MLA_V)), MLA_KV_RANK),
        "mla_g_q": gain(ks[20], (NM, MLA_QK)),
        "mla_g_k": gain(ks[21], (NM, MLA_QK)),
        "mla_w_out": nrm(ks[22], (NM, MLA_HEADS * MLA_V, D_MODEL), MLA_HEADS * MLA_V),
    }


def reference(x, p, positions, g_mix, g_mlp, g_ple, w1, w2, w_gate, w_ple,
              diff_w_in, diff_w_out, diff_g_q, diff_g_k, diff_lambda, diff_g_sub,
              mla_w_in, mla_g_cq, mla_g_ckv, mla_w_uq, mla_w_ukv, mla_g_q, mla_g_k, mla_w_out):
    cos_d, sin_d = rope_tables(positions, DIFF_ROT)
    cos_m, sin_m = rope_tables(positions, MLA_ROPE)
    for i in range(DEPTH):
        j = i // N_MIXERS
        h = rms_norm(x, g_mix[i])
        if i % N_MIXERS == 0:
            lambda_init = 0.8 - 0.6 * math.exp(-0.3 * i)
            mix = diff_attention(h, diff_w_in[j], diff_w_out[j], diff_g_q[j], diff_g_k[j],
                                 diff_lambda[j], diff_g_sub[j], cos_d, sin_d, lambda_init)
        else:
            mix = mla_attention(h, mla_w_in[j], mla_g_cq[j], mla_g_ckv[j], mla_w_uq[j],
                                mla_w_ukv[j], mla_g_q[j], mla_g_k[j], mla_w_out[j], cos_m, sin_m)
        x = x + mix
        x = x + sq_relu_mlp(rms_norm(x, g_mlp[i]), w1[i], w2[i])
        gate = jax.nn.sigmoid(rms_norm(x, g_ple[i]) @ w_gate[i])
        x = x + gate * (p[i] @ w_ple[i])
    return x
```

```python
import math
from collections import defaultdict
from contextlib import ExitStack

import numpy as np
import concourse.bass as bass
import concourse.mybir as mybir
from concourse.bass_utils import run_bass_kernel_spmd

F32 = mybir.dt.float32
BF16 = mybir.dt.bfloat16
I32 = mybir.dt.int32
AF = mybir.ActivationFunctionType
ALU = mybir.AluOpType
EPS = 1e-6
THETA = 500000.0


class StopBuild(Exception):
    pass


class Cfg:
    stop = None
    def __init__(self, D=4096, T=4096, TB=1024, DFF=16384, depth=4, diff_heads=16,
                 mla_heads=32, q_rank=1024, kv_rank=512, ple=256, n_cores=2):
        self.D, self.T, self.TB, self.DFF, self.depth = D, T, TB, DFF, depth
        self.diff_heads, self.mla_heads = diff_heads, mla_heads
        self.q_rank, self.kv_rank, self.ple = q_rank, kv_rank, ple
        self.n_cores = n_cores
        self.DC = D // 128
        self.NTB = T // TB
        self.NH = TB // 512
        self.HB = 8


class LV:
    i = 0
    cache = {}


def dcol(ap, mult, off, n, fused=False):
    assert isinstance(LV.i, int)
    st = LV.i * mult + off
    return ap[:, st:st + n]


def drow(ap, mult, off, n):
    assert isinstance(LV.i, int)
    st = LV.i * mult + off
    return ap[st:st + n, :]


class _MockIns:
    def then_inc(self, *a, **k):
        return self


class CountEng:
    def __init__(self):
        self.n = 0

    def __getattr__(self, name):
        def w(*a, **k):
            self.n += 1
            return _MockIns()
        return w


class SerialEng:
    def __init__(self, eng):
        self._e = eng
        self.wv = None
        self.sem = None
        self.k = 0

    def wait_ge(self, *a, **k):
        return self._e.wait_ge(*a, **k)

    def __getattr__(self, name):
        f = getattr(self._e, name)

        def w(*a, **k):
            if self.k:
                self.wv(self.k)
            ins = f(*a, **k)
            ins.then_inc(self.sem, 1)
            self.k += 1
            return ins
        return w

    def finish_stage(self, sem):
        self.wv(self.k)
        self._e.sem_inc(sem, 1)


class Sched:
    COMPUTE = ("pe", "act", "dve")
    DMAQ = ("pool", "sp")
    SELF = ("self_act", "self_dve")

    def __init__(self):
        self.st = defaultdict(lambda: defaultdict(list))
        self.ndma = defaultdict(lambda: defaultdict(int))
        self.maxs = -1
        self.regions = {}

    def at(self, s, eng, fn, ndma=0):
        self.st[s][eng].append(fn)
        if eng in self.DMAQ:
            assert ndma > 0
            self.ndma[s][eng] += ndma
        self.maxs = max(self.maxs, s)

    def next(self):
        return self.maxs + 1

    def region(self, b, e, n):
        assert e > b
        self.regions[b] = (e, n)

    def emit(self, nc, block, sems):
        S = self.maxs + 1
        allx = self.COMPUTE + self.DMAQ + self.SELF
        nins = {"act": [0] * S, "dve": [0] * S}
        LV.i = 0
        for e in ("act", "dve"):
            for s in range(S):
                fns = self.st[s].get(e)
                if fns:
                    ce = CountEng()
                    for fn in fns:
                        fn(ce)
                    nins[e][s] = ce.n
        inc = {x: [0] * S for x in allx}
        for s in range(S):
            for e in self.COMPUTE:
                inc[e][s] = 1 if self.st[s].get(e) else 0
            for e in self.DMAQ:
                inc[e][s] = 16 * self.ndma[s].get(e, 0)
            inc["self_act"][s] = nins["act"][s]
            inc["self_dve"][s] = nins["dve"][s]
        reg_of = [None] * S
        delta = {}
        for b, (e_, n) in self.regions.items():
            delta[b] = {x: sum(inc[x][b:e_]) for x in allx}
            for s in range(b, e_):
                assert reg_of[s] is None
                reg_of[s] = b
        cum = {x: [0] * (S + 1) for x in allx}
        for s in range(S):
            for x in allx:
                cum[x][s + 1] = cum[x][s] + inc[x][s]
            b = reg_of[s]
            if b is not None and s + 1 == self.regions[b][0]:
                n = self.regions[b][1]
                for x in allx:
                    cum[x][s + 1] += (n - 1) * delta[b][x]
        names = {"pe": "tensor", "act": "scalar", "dve": "vector", "pool": "gpsimd", "sp": "sync"}

        def make(e):
            def body(raw):
                eng = SerialEng(raw) if e in ("act", "dve") else raw
                selfx = "self_" + e
                if e in ("act", "dve"):
                    eng.sem = sems[selfx]
                waited = {x: -1 for x in self.COMPUTE + self.DMAQ}
                st8 = {"scr": None, "i": None, "it": None}

                def wait(x, v, b):
                    if b is not None and delta[b][x]:
                        if st8["it"] is not None:
                            raw.wait_ge(sems[x], v + st8["it"] * delta[b][x])
                        else:
                            raw.reg_add(st8["scr"], st8["base"][x], v)
                            raw.wait_ge(sems[x], st8["scr"])
                    else:
                        raw.wait_ge(sems[x], v)

                def emit_stage(s, b):
                    fns = self.st[s].get(e)
                    if not fns:
                        return
                    for x in self.COMPUTE + self.DMAQ:
                        if x == e and e in self.COMPUTE:
                            continue
                        tgt = cum[x][s]
                        if tgt > waited[x]:
                            wait(x, tgt, b)
                            waited[x] = tgt
                    if e in ("act", "dve"):
                        eng.wv = lambda k, s=s, b=b: wait(selfx, cum[selfx][s] + k, b)
                        eng.k = 0
                    last = None
                    cnt = 0
                    for fn in fns:
                        r = fn(eng)
                        if e in self.DMAQ:
                            for ins in r:
                                ins.then_inc(sems[e], 16)
                                cnt += 1
                        else:
                            last = r
                    if e in ("act", "dve"):
                        assert eng.k == nins[e][s], (e, s, eng.k, nins[e][s])
                        eng.finish_stage(sems[e])
                    elif e in self.COMPUTE:
                        last.then_inc(sems[e], 1)
                    else:
                        assert cnt == self.ndma[s][e], (s, e, cnt, self.ndma[s][e])

                s = 0
                while s < S:
                    if s in self.regions:
                        e_, n = self.regions[s]
                        if any(self.st[ss].get(e) for ss in range(s, e_)):
                            for x in waited:
                                waited[x] = -1
                            if e in self.DMAQ:
                                for it in range(n):
                                    LV.i = it
                                    st8["it"] = it
                                    for x in waited:
                                        waited[x] = -1
                                    for ss in range(s, e_):
                                        emit_stage(ss, s)
                                st8["it"] = None
                                LV.i = 0
                            else:
                                if st8["scr"] is None:
                                    st8["scr"] = raw.alloc_register("ls_%s" % e)
                                    st8["base"] = {x: raw.alloc_register("lb_%s_%s" % (e, x)) for x in allx
                                                   if x in self.COMPUTE + self.DMAQ or x == selfx}
                                xs = [x for x in st8["base"] if delta[s][x]]
                                for x in xs:
                                    raw.reg_mov(st8["base"][x], 0)
                                with raw.Fori(0, n) as i:
                                    LV.i = i
                                    st8["i"] = i
                                    for ss in range(s, e_):
                                        emit_stage(ss, s)
                                    for x in xs:
                                        raw.reg_add(st8["base"][x], st8["base"][x], delta[s][x])
                                LV.i = 0
                            for x in waited:
                                waited[x] = -1
                        s = e_
                    else:
                        emit_stage(s, None)
                        s += 1
                for x in self.COMPUTE + self.DMAQ:
                    if x == e and e in self.COMPUTE:
                        continue
                    raw.wait_ge(sems[x], cum[x][S])
            return body

        for e in self.COMPUTE + self.DMAQ:
            getattr(block, names[e])(make(e))


def tile_w(W, cols):
    K = W.shape[0]
    KC = K // 128
    cols = np.asarray(cols)
    nch = cols.shape[0]
    Wg = W[:, cols.reshape(-1)].reshape(KC, 128, nch, 128)
    return np.ascontiguousarray(Wg.transpose(2, 1, 0, 3))


def tile_w_tm(W):
    K, N = W.shape
    KC = K // 128
    return np.ascontiguousarray(W.reshape(KC, 128, N // 256, 256).transpose(2, 1, 0, 3))


def col_layout(g, n):
    return np.ascontiguousarray(np.asarray(g).reshape(n, 128).T)


def const_tables(cfg):
    c = {}
    c["ones"] = np.ones((128, 128), np.float32)
    blk = np.zeros((128, 128), np.float32)
    blk[:64, :64] = 1
    blk[64:, 64:] = 1
    c["ones64"] = blk
    pd = np.zeros((128, 128), np.float32)
    for m in range(16):
        pd[m + 16, m] = 1
        pd[m, m + 16] = 1
    c["perm_d"] = pd
    pm = np.zeros((128, 128), np.float32)
    for b in (0, 64):
        for m in range(32):
            pm[b + m + 32, b + m] = 1
            pm[b + m, b + m + 32] = 1
    c["perm_m"] = pm
    col = np.zeros((128, 8), np.float32)
    inv_d = (THETA ** (-np.arange(0, 32, 2, dtype=np.float32) / np.float32(32))).astype(np.float32)
    inv_m = (THETA ** (-np.arange(0, 64, 2, dtype=np.float32) / np.float32(64))).astype(np.float32)
    for p in range(128):
        if p < 32:
            col[p, 0] = inv_d[p % 16]
            col[p, 1] = 1.0
            col[p, 2] = 0.0
            col[p, 3] = -1.0 if p < 16 else 1.0
        else:
            col[p, 1] = 0.0
            col[p, 2] = 1.0
            col[p, 3] = 0.0
        q = p % 64
        col[p, 4] = inv_m[q % 32]
        col[p, 5] = 1.0
        col[p, 6] = 0.0
        col[p, 7] = -1.0 if q < 32 else 1.0
    c["cols"] = col
    return c


def build(cfg):
    nc = bass.Bass("TRN2", target_bir_lowering=False)
    D, T, TB, DC, NTB, NH, DFF = cfg.D, cfg.T, cfg.TB, cfg.DC, cfg.NTB, cfg.NH, cfg.DFF
    FC = DFF // 128
    HB = cfg.HB
    NHB = FC // HB
    ND = (cfg.depth + 1) // 2
    NM = cfg.depth // 2
    HD = cfg.diff_heads
    HM = cfg.mla_heads
    QC = cfg.q_rank // 128
    KVC = cfg.kv_rank // 128
    PC = cfg.ple // 128
    KT = T // 128
    NQB = T // 512

    def din(name, shape, dt=F32):
        return nc.dram_tensor(name, list(shape), dt, kind="ExternalInput").ap()

    def dscr(name, shape, dt):
        if getattr(cfg, "dump", False):
            return nc.dram_tensor(name, list(shape), dt, kind="ExternalOutput").ap()
        return nc.dram_tensor(name, list(shape), dt).ap()

    xT = din("xT", [D, T])
    pT = din("pT", [cfg.depth * cfg.ple, T])
    posb = din("posb", [128, T], I32)
    gcols = din("gcols", [128, 3 * cfg.depth * DC])
    consts = din("consts", [128, 4 * 128 + 8])
    w1 = din("w1", [cfg.depth * FC, 128, DC, 128])
    w2 = din("w2", [cfg.depth * NHB * DC, 128, HB, 128])
    wg = din("wg", [cfg.depth * DC, 128, DC + PC, 128])
    d_wqk = din("d_wqk", [ND * 4 * HD, 128, DC, 128])
    d_wv = din("d_wv", [ND * HD, 128, DC, 256])
    d_wo = din("d_wo", [ND * DC, 128, 2 * HD, 128])
    d_g = din("d_g", [128, ND * 8])
    m_win = din("m_win", [NM * (QC + KVC + 1), 128, DC, 128])
    m_wuq = din("m_wuq", [NM * (HM + HM // 2), 128, QC, 128])
    m_wuk = din("m_wuk", [NM * HM, 128, KVC, 128])
    m_wuv = din("m_wuv", [NM * (HM // 2), 128, KVC, 256])
    m_wo = din("m_wo", [NM * DC, 128, HM, 128])
    m_g = din("m_g", [128, NM * (QC + KVC + 4)])
    outT = nc.dram_tensor("outT", [D, T], F32, kind="ExternalOutput").ap()

    QW = max(2 * HD, HM + HM // 2) * 128
    KW = max(2 * HD, HM + 1) * 128
    qTd = dscr("qTd", [QW, T], BF16)
    kTd = dscr("kTd", [KW, T], BF16)
    vd = dscr("vd", [T, D], BF16)
    oTd = dscr("oTd", [D, T], BF16)
    latd = dscr("latd", [(QC + KVC) * 128, TB], F32)
    tabd = dscr("tabd", [4 * 128, T], F32)
    Xblk = dscr("Xblk", [D, TB], F32)
    qblk = dscr("qblk", [QW, TB], BF16)
    kblk = dscr("kblk", [KW, TB], BF16)
    vblk = dscr("vblk", [TB, D], BF16)
    tabblk = dscr("tabblk", [4 * 128, TB], F32)
    oblk = dscr("oblk", [D, TB], BF16)
    pblk = dscr("pblk", [cfg.ple, TB], F32)
    qu = dscr("qu", [3 * 128, T], BF16)
    ku = dscr("ku", [3 * 128, T], BF16)
    vu = dscr("vu", [T, 256], BF16)
    ou = dscr("ou", [256, T], BF16)

    es = ExitStack()
    with es:
        def sb(name, shape, dt):
            return es.enter_context(nc.sbuf_tensor(name, list(shape), dt))

        cst = sb("cst", [128, 4 * 128 + 8], F32)
        cstb = sb("cstb", [128, 4 * 128], BF16)
        gsb = sb("gsb", [128, 3 * cfg.depth * DC], F32)
        dg = sb("dg", [128, ND * 8], F32)
        mg = sb("mg", [128, NM * (QC + KVC + 4)], F32)
        lamc = sb("lamc", [128, ND * 4], F32)
        arena = sb("arena", [128, 32768], BF16)
        arenb = sb("arenb", [128, 8192], BF16)

        def view(ar, off, shape, dt):
            n = 1
            for d in shape:
                n *= d
            esz = 4 if dt in (F32, I32) else 2
            a = ar[:, off // 2: off // 2 + n * esz // 2]
            if dt != BF16:
                a = a.bitcast(dt)
            if len(shape) == 2:
                a = a.rearrange("p (a b) -> p a b", b=shape[1])
            elif len(shape) == 3:
                a = a.rearrange("p (a b c) -> p a b c", b=shape[1], c=shape[2])
            return a, off + n * esz

        hT, _ = view(arena, 0, [DC, TB], BF16)
        o = 0
        kbuf, o = view(arena, o, [2, T], BF16)
        vbuf, o = view(arena, o, [KT, 256], BF16)
        qbuf, o = view(arena, o, [2, 2, 512], BF16)
        pbuf, o = view(arena, o, [2, 2, 512], BF16)
        osq, o = view(arena, o, [2, 512], BF16)
        oout, o = view(arena, o, [2, 2, 512], BF16)
        ob, o = view(arena, o, [2, 2, 512], F32)
        rsum, o = view(arena, o, [512], F32)
        of, o = view(arena, o, [2, 512], F32)
        orst, o = view(arena, o, [512], F32)
        assert o <= 65536, o
        o = 0
        posi, o = view(arena, o, [512], I32)
        tmpA, o = view(arena, o, [2, 512], F32)
        tmpB, o = view(arena, o, [2, 512], F32)
        tmpC, o = view(arena, o, [512], F32)
        tmpI, o = view(arena, o, [512], I32)
        o = 0
        xbuf, o = view(arenb, o, [2, TB], F32)
        sqb, o = view(arenb, o, [2, TB], BF16)
        rstd, o = view(arenb, o, [TB], F32)
        assert o <= 16384, o
        o = 0
        ep_f, o = view(arenb, o, [2, 1024], F32)
        ep_g, o = view(arenb, o, [2, 512], F32)
        ep_r, o = view(arenb, o, [2, 512], F32)
        assert o <= 16384, o
        wbuf = sb("wbuf", [128, 2, 8192], BF16)
        ep_s = sb("ep_s", [128, 2, 512], BF16)
        ep_z = sb("ep_z", [128, 2, 512], BF16)
        ep_o = sb("ep_o", [128, 2, 1024], BF16)
        tbuf = sb("tbuf", [128, 2, 2, 512], F32)
        ep_g3 = sb("ep_g3", [128, 3, 512], F32)
        aT = sb("aT", [128, 2, HB, TB], BF16)
        pTs = sb("pTs", [128, PC, TB], BF16)

        ps = es.enter_context(nc.psum_tensor("ps", [128, 8, 512], F32))

        sems = {e: es.enter_context(nc.semaphore("sem_" + e)) for e in Sched.COMPUTE + Sched.DMAQ}
        sems["self_act"] = es.enter_context(nc.semaphore("sem_self_act"))
        sems["self_dve"] = es.enter_context(nc.semaphore("sem_self_dve"))
        block = es.enter_context(nc.Block())
        sc = Sched()

        ONES, ONES64, PERMD, PERMM = (cstb[:, i * 128:(i + 1) * 128] for i in range(4))
        CC = lambda i: cst[:, 512 + i:512 + i + 1]

        s = 0
        sc.at(s, "sp", lambda e: [e.dma_start(out=cst[:, :], in_=consts[:, :]),
                                  e.dma_start(out=gsb[:, :], in_=gcols[:, :]),
                                  e.dma_start(out=dg[:, :], in_=d_g[:, :]),
                                  e.dma_start(out=mg[:, :], in_=m_g[:, :]),
                                  e.dma_start(out=outT[:, :], in_=xT[:, :])], ndma=5)
        s += 1
        sc.at(s, "dve", lambda e: e.tensor_copy(out=cstb[:, :], in_=cst[:, 0:512]))
        for j in range(ND):
            li = 0.8 - 0.6 * math.exp(-0.3 * (2 * j))
            b = j * 8

            def f_l1(e, b=b, j=j):
                e.tensor_tensor(out=tmpA[:, 0, 0:1], in0=dg[:, b + 4:b + 5], in1=dg[:, b + 5:b + 6], op=ALU.mult)
                return e.tensor_tensor(out=tmpA[:, 0, 1:2], in0=dg[:, b + 6:b + 7], in1=dg[:, b + 7:b + 8], op=ALU.mult)

            def f_l2(e):
                return e.matmul(ps[:, 7, 0:2], lhsT=cst[:, 0:128], rhs=tmpA[:, 0, 0:2], start=True, stop=True)

            def f_l3(e):
                return e.activation(out=tmpA[:, 1, 0:2], in_=ps[:, 7, 0:2], func=AF.Exp)

            def f_l4(e, j=j, li=li):
                e.tensor_tensor(out=tmpA[:, 1, 2:3], in0=tmpA[:, 1, 1:2], in1=tmpA[:, 1, 0:1], op=ALU.subtract)
                return e.tensor_single_scalar(out=lamc[:, j * 4:j * 4 + 1], in_=tmpA[:, 1, 2:3], scalar=-li, op=ALU.add)

            sc.at(s + 1, "dve", f_l1)
            sc.at(s + 2, "pe", f_l2)
            sc.at(s + 3, "act", f_l3)
            sc.at(s + 4, "dve", f_l4)
            s += 4
        s = sc.next()
        for cb in range(T // 512):
            sl = slice(cb * 512, (cb + 1) * 512)
            sc.at(s, "sp", lambda e, sl=sl: [e.dma_start(out=posi[:, :], in_=posb[:, sl])], ndma=1)
            for ti, base in ((0, 0), (1, 4)):

                def f_t1(e, base=base):
                    TWO_PI = 2 * math.pi
                    e.tensor_copy(out=tmpA[:, 0, :], in_=posi[:, :])
                    e.tensor_scalar(out=tmpA[:, 0, :], in0=tmpA[:, 0, :], scalar1=CC(base), scalar2=None, op0=ALU.mult)
                    r = None
                    for dsti, add in ((1, math.pi / 2), (0, 0.0)):
                        dst = tmpA[:, dsti, :]
                        e.tensor_single_scalar(out=dst, in_=tmpA[:, 0, :], scalar=add, op=ALU.add)
                        e.tensor_single_scalar(out=tmpC[:, :], in_=dst, scalar=1.0 / TWO_PI, op=ALU.mult)
                        e.tensor_copy(out=tmpI[:, :], in_=tmpC[:, :])
                        e.tensor_copy(out=tmpC[:, :], in_=tmpI[:, :])
                        e.tensor_single_scalar(out=tmpC[:, :], in_=tmpC[:, :], scalar=-TWO_PI, op=ALU.mult)
                        e.tensor_tensor(out=dst, in0=dst, in1=tmpC[:, :], op=ALU.add)
                        e.tensor_single_scalar(out=tmpC[:, :], in_=dst, scalar=math.pi, op=ALU.is_gt)
                        e.tensor_single_scalar(out=tmpC[:, :], in_=tmpC[:, :], scalar=-TWO_PI, op=ALU.mult)
                        r = e.tensor_tensor(out=dst, in0=dst, in1=tmpC[:, :], op=ALU.add)
                    return r

                def f_t2(e):
                    e.activation(out=tmpA[:, 0, :], in_=tmpA[:, 0, :], func=AF.Sin)
                    return e.activation(out=tmpA[:, 1, :], in_=tmpA[:, 1, :], func=AF.Sin)

                def f_t3(e, base=base):
                    e.tensor_scalar(out=tmpB[:, 0, :], in0=tmpA[:, 1, :], scalar1=CC(base + 1), scalar2=CC(base + 2),
                                    op0=ALU.mult, op1=ALU.add)
                    return e.tensor_scalar(out=tmpB[:, 1, :], in0=tmpA[:, 0, :], scalar1=CC(base + 3), scalar2=None,
                                           op0=ALU.mult)

                def f_t4(e, ti=ti, sl=sl):
                    return [e.dma_start(out=tabd[(2 * ti) * 128:(2 * ti + 1) * 128, sl], in_=tmpB[:, 0, :]),
                            e.dma_start(out=tabd[(2 * ti + 1) * 128:(2 * ti + 2) * 128, sl], in_=tmpB[:, 1, :])]

                o = 1 + 3 * ti
                sc.at(s + o, "dve", f_t1)
                sc.at(s + o + 1, "act", f_t2)
                sc.at(s + o + 2, "dve", f_t3)
                sc.at(s + o + 3, "sp", f_t4, ndma=2)
            s += 8

        class _TokOff:
            @property
            def v(self):
                return LV.i * TB

        TOK0 = _TokOff()
        TBM = 0

        def norm_phase(s0, src, nchunk, tok0, gcol_fn, dst):
            nfeat = nchunk * 128
            s = s0
            for c in range(nchunk):
                par = c % 2

                def ld(e, c=c, par=par):
                    return [e.dma_start(out=xbuf[:, par, :], in_=dcol(src[c * 128:(c + 1) * 128, :], TBM, 0, TB))]

                def sq(e, par=par):
                    return e.activation(out=sqb[:, par, :], in_=xbuf[:, par, :], func=AF.Square)

                def mm(e, c=c, par=par):
                    r = None
                    for h in range(NH):
                        r = e.matmul(ps[:, 4 + h, :], lhsT=ONES, rhs=sqb[:, par, h * 512:(h + 1) * 512],
                                     start=(c == 0), stop=(c == nchunk - 1))
                    return r

                sc.at(s + c, "sp", ld, ndma=1)
                sc.at(s + c + 1, "act", sq)
                sc.at(s + c + 2, "pe", mm)
            s = s + nchunk + 2

            def rs1(e):
                r = None
                for h in range(NH):
                    r = e.activation(out=rstd[:, h * 512:(h + 1) * 512], in_=ps[:, 4 + h, :], func=AF.Sqrt,
                                     bias=CC(8 - 8) if False else epsc[:, 0:1], scale=1.0 / nfeat)
                return r

            def rs2(e):
                return e.reciprocal(out=rstd[:, :], in_=rstd[:, :])

            sc.at(s, "act", rs1)
            sc.at(s + 1, "dve", rs2)
            for c in range(nchunk):
                par = c % 2

                def ld(e, c=c, par=par):
                    return [e.dma_start(out=xbuf[:, par, :], in_=dcol(src[c * 128:(c + 1) * 128, :], TBM, 0, TB))]

                def scl(e, c=c, par=par):
                    return e.scalar_tensor_tensor(out=dst[:, c, :], in0=xbuf[:, par, :], scalar=gcol_fn(c),
                                                  in1=rstd[:, :], op0=ALU.mult, op1=ALU.mult)

                sc.at(s + 1 + c, "sp", ld, ndma=1)
                sc.at(s + 2 + c, "dve", scl)
            return sc.next()

        def gemm_fm(s0, wd, wbase, njobs, KC, inT, tokw, epi, nh_per_task=1, extra_kc=0, extra_in=None, psb=0):
            ntask_per_job = tokw // (512 * nh_per_task)
            KCT = KC + extra_kc
            s = s0
            ti = 0
            for j in range(njobs):
                wpar = j % 2
                wv = wbuf[:, wpar, 0:KCT * 128]

                def ldw(e, j=j, wv=wv):
                    return [e.dma_start(out=wv, in_=wd[wbase + j].rearrange("p k m -> p (k m)"))]

                sc.at(s + j * ntask_per_job, "pool", ldw, ndma=1)
                for tt in range(ntask_per_job):
                    st = s + j * ntask_per_job + tt + 1
                    ppar = ti % 2
                    banks = [psb + ppar * nh_per_task + h for h in range(nh_per_task)]
                    t0 = tt * 512 * nh_per_task

                    def mm(e, wpar=wpar, banks=banks, t0=t0, ppar=ppar):
                        r = None
                        for h, bk in enumerate(banks):
                            for kc in range(KC):
                                r = e.matmul(ps[:, bk, :], lhsT=wbuf[:, wpar, kc * 128:(kc + 1) * 128],
                                             rhs=inT[:, kc, t0 + h * 512:t0 + (h + 1) * 512],
                                             start=(kc == 0), stop=(kc == KC - 1))
                        if extra_kc:
                            for kc in range(extra_kc):
                                r = e.matmul(ps[:, 4 + ppar, :], lhsT=wbuf[:, wpar, (KC + kc) * 128:(KC + kc + 1) * 128],
                                             rhs=extra_in[:, kc, t0:t0 + 512], start=(kc == 0), stop=(kc == extra_kc - 1))
                        return r

                    if getattr(cfg, "dbg", 0) != 1:
                        sc.at(st, "pe", mm)
                    if getattr(cfg, "dbg", 0) not in (1, 2):
                        epi(st + 1, j, t0, 512 * nh_per_task, [ps[:, bk, :] for bk in banks], ti)
                    ti += 1
            return sc.next()

        def epi_qk(dst, row_fn, tok0, gcol_fn, ones_m, perm_m, tab_i):
            def epi(st, j, t0, ntok, pl, ti):
                par = ti % 2
                g3 = ti % 3
                p = pl[0]

                def e1a(e):
                    e.activation(out=ep_s[:, par, :], in_=p, func=AF.Square)
                    r = e.activation(out=ep_g3[:, g3, :], in_=p, func=AF.Identity, scale=gcol_fn(j))
                    if perm_m is not None:
                        r = e.activation(out=ep_z[:, par, :], in_=p, func=AF.Identity, scale=gcol_fn(j))
                    return r

                def e2(e):
                    r = e.matmul(ps[:, 4 + par, :], lhsT=ones_m, rhs=ep_s[:, par, :], start=True, stop=True)
                    if perm_m is not None:
                        r = e.matmul(ps[:, 6 + par, :], lhsT=perm_m, rhs=ep_z[:, par, :], start=True, stop=True)
                    return r

                def e3a(e):
                    return e.activation(out=ep_r[:, par, :], in_=ps[:, 4 + par, :], func=AF.Sqrt,
                                        bias=epsc[:, 0:1], scale=1.0 / (128.0 if ones_m is ONES else 64.0))

                def e3d(e):
                    if perm_m is None:
                        return e.tensor_copy(out=ep_f[:, par, 0:512], in_=ep_g3[:, g3, :])
                    e.tensor_tensor(out=ep_f[:, par, 0:512], in0=ps[:, 6 + par, :], in1=tbuf[:, par, 1, :], op=ALU.mult)
                    e.tensor_tensor(out=ep_f[:, par, 512:1024], in0=ep_g3[:, g3, :], in1=tbuf[:, par, 0, :], op=ALU.mult)
                    return e.tensor_tensor(out=ep_f[:, par, 0:512], in0=ep_f[:, par, 0:512], in1=ep_f[:, par, 512:1024], op=ALU.add)

                def e4(e):
                    e.reciprocal(out=ep_r[:, par, :], in_=ep_r[:, par, :])
                    return e.tensor_tensor(out=ep_o[:, par, 0:512], in0=ep_f[:, par, 0:512], in1=ep_r[:, par, :], op=ALU.mult)

                def e5(e):
                    r0 = row_fn(j)
                    return [e.dma_start(out=dcol(dst[r0:r0 + 128, :], TBM, t0, 512), in_=ep_o[:, par, 0:512])]

                def eld(e):
                    return [e.dma_start(out=tbuf[:, par, 0, :], in_=dcol(tabblk[(2 * tab_i) * 128:(2 * tab_i + 1) * 128, :], TBM, t0, 512)),
                            e.dma_start(out=tbuf[:, par, 1, :], in_=dcol(tabblk[(2 * tab_i + 1) * 128:(2 * tab_i + 2) * 128, :], TBM, t0, 512))]

                skip = getattr(cfg, "skip", ())
                if "e1a" not in skip:
                    sc.at(st, "act", e1a)
                if perm_m is not None and "eld" not in skip:
                    sc.at(st + 1, "sp", eld, ndma=2)
                if "e2" not in skip:
                    sc.at(st + 1, "pe", e2)
                if "e3a" not in skip:
                    sc.at(st + 2, "act", e3a)
                if "e3d" not in skip:
                    sc.at(st + 2, "dve", e3d)
                if "e4" not in skip:
                    sc.at(st + 3, "dve", e4)
                if "e5" not in skip:
                    sc.at(st + 4, "sp", e5, ndma=1)
            return epi

        def epi_store_f32(dst, row_fn, tok0):
            def epi(st, j, t0, ntok, pl, ti):
                par = ti % 2

                def c(e):
                    return e.activation(out=ep_f[:, par, 0:512], in_=pl[0], func=AF.Identity)

                def d(e):
                    r0 = row_fn(j)
                    return [e.dma_start(out=dcol(dst[r0:r0 + 128, :], TBM, t0, 512), in_=ep_f[:, par, 0:512])]

                dbg = getattr(cfg, "dbg", 0)
                if dbg == 6:
                    sc.at(st, "dve", lambda e: e.tensor_copy(out=ep_f[:, par, 0:512], in_=pl[0]))
                elif dbg != 4:
                    sc.at(st, "act", c)
                if dbg != 3:
                    sc.at(st + (2 if dbg == 5 else 1), "sp", d, ndma=1)
            return epi

        def epi_accum(tok0):
            def epi(st, j, t0, ntok, pl, ti):
                par = ti % 2

                def c(e):
                    r = None
                    for h, p in enumerate(pl):
                        r = e.activation(out=ep_f[:, par, h * 512:(h + 1) * 512], in_=p, func=AF.Identity)
                    return r

                def d(e):
                    return [e.dma_start(out=dcol(Xblk[j * 128:(j + 1) * 128, :], TBM, t0, ntok, fused=True), in_=ep_f[:, par, 0:ntok],
                                        accum_op=ALU.add)]

                sc.at(st, "act", c)
                sc.at(st + 1, "pool", d, ndma=1)
            return epi

        def epi_relu2(aset):
            def epi(st, j, t0, ntok, pl, ti):
                par = ti % 2

                def c(e):
                    return e.activation(out=ep_f[:, par, 0:512], in_=pl[0], func=AF.Relu)

                def d(e):
                    return e.tensor_tensor(out=aT[:, aset, j, t0:t0 + 512], in0=ep_f[:, par, 0:512], in1=ep_f[:, par, 0:512], op=ALU.mult)

                sc.at(st, "act", c)
                sc.at(st + 1, "dve", d)
            return epi

        def epi_gate(tok0):
            def epi(st, j, t0, ntok, pl, ti):
                par = ti % 2

                def c(e):
                    return e.activation(out=ep_g[:, par, :], in_=pl[0], func=AF.Sigmoid)

                def c2(e):
                    return e.tensor_copy(out=ep_r[:, par, :], in_=ps[:, 4 + par, :])

                def m(e):
                    return e.tensor_tensor(out=ep_f[:, par, 0:512], in0=ep_g[:, par, :], in1=ep_r[:, par, :], op=ALU.mult)

                def d(e):
                    return [e.dma_start(out=dcol(Xblk[j * 128:(j + 1) * 128, :], TBM, t0, 512, fused=True), in_=ep_f[:, par, 0:512], accum_op=ALU.add)]

                sc.at(st, "act", c)
                sc.at(st, "dve", c2)
                sc.at(st + 1, "dve", m)
                sc.at(st + 2, "pool", d, ndma=1)
            return epi

        def gemm_tm(s0, wd, wbase, ncb, KC, inT, tok0, dst, col0):
            s = s0
            ntt = TB // 128
            ti = 0
            for cb in range(ncb):
                wpar = cb % 2

                def ldw(e, cb=cb, wpar=wpar):
                    return [e.dma_start(out=wbuf[:, wpar, 0:KC * 256], in_=wd[wbase + cb].rearrange("p k m -> p (k m)"))]

                sc.at(s + cb * ntt, "pool", ldw, ndma=1)
                for tt in range(ntt):
                    st = s + cb * ntt + tt + 1
                    par = ti % 2

                    def mm(e, wpar=wpar, tt=tt, par=par):
                        r = None
                        for kc in range(KC):
                            r = e.matmul(ps[:, par, 0:256], lhsT=inT[:, kc, tt * 128:(tt + 1) * 128],
                                         rhs=wbuf[:, wpar, kc * 256:(kc + 1) * 256], start=(kc == 0), stop=(kc == KC - 1))
                        return r

                    def c(e, par=par):
                        return e.activation(out=ep_o[:, par, 0:256], in_=ps[:, par, 0:256], func=AF.Identity)

                    def d(e, par=par, tt=tt, cb=cb):
                        return [e.dma_start(out=drow(dst[:, col0 + cb * 256:col0 + (cb + 1) * 256], TBM, tt * 128, 128), in_=ep_o[:, par, 0:256])]

                    sc.at(st, "pe", mm)
                    sc.at(st + 1, "act", c)
                    sc.at(st + 2, "sp", d, ndma=1)
                    ti += 1
            return sc.next()

        def attention(s0, nunits, hpu, nmaps, vw, qrow_fn, krow_fn, vcol_fn, orow_fn, scale, cin, cout, diff_j=None, gsub_fn=None, li=0.0):
            EC = vw // 128
            G = 2
            s = s0
            rb = s
            sc.at(s, "sp", lambda e: [e.dma_start(out=d_, in_=f_()) for (d_, f_) in cin], ndma=len(cin))
            s += 1
            for hh in range(hpu):
                def ldk(e, hh=hh):
                    r = []
                    for m in range(nmaps):
                        for pi, (mult, off, nr, p0) in enumerate(krow_fn(hh, m)):
                            idx = m if nmaps == 2 else pi
                            r.append(e.dma_start(out=kbuf[p0:p0 + nr, idx, :], in_=drow(ku, mult, off, nr)))
                    vm, vo = vcol_fn(hh)
                    r.append(e.dma_start(out=vbuf[:, :, 0:vw], in_=dcol(vu, vm, vo, vw).rearrange("(kt p) e -> p kt e", p=128)))
                    return r

                nk = sum(len(krow_fn(hh, m)) for m in range(nmaps)) + 1
                sc.at(s, "sp", ldk, ndma=nk)
                s += 1
                for qb in range(NQB):
                    qpar = qb % 2
                    q0 = qb * 512

                    def ldq(e, hh=hh, qpar=qpar, q0=q0):
                        r = []
                        for m in range(nmaps):
                            for pi, (mult, off, nr, p0) in enumerate(qrow_fn(hh, m)):
                                idx = m if nmaps == 2 else pi
                                r.append(e.dma_start(out=qbuf[p0:p0 + nr, qpar, idx, :], in_=drow(qu[:, q0:q0 + 512], mult, off, nr)))
                        return r

                    nq = sum(len(qrow_fn(hh, m)) for m in range(nmaps))
                    sc.at(s, "sp", ldq, ndma=nq)
                    s += 1
                    for m in range(nmaps):
                        parts_q = [(nr, p0) for (_, _, nr, p0) in qrow_fn(hh, m)]
                        ntask = KT // G
                        for tk in range(ntask):
                            spar = tk % 2

                            def smm(e, tk=tk, spar=spar, m=m, parts_q=parts_q, qpar=qpar):
                                r = None
                                for g in range(G):
                                    kt = tk * G + g
                                    for pi, (nr, p0) in enumerate(parts_q):
                                        idx = m if nmaps == 2 else pi
                                        r = e.matmul(ps[:, spar * 2 + g, :], lhsT=kbuf[p0:p0 + nr, idx, kt * 128:(kt + 1) * 128],
                                                     rhs=qbuf[p0:p0 + nr, qpar, idx, :], start=(pi == 0), stop=(pi == len(parts_q) - 1))
                                return r

                            def ex(e, spar=spar):
                                r = None
                                for g in range(G):
                                    r = e.activation(out=pbuf[:, spar, g, :], in_=ps[:, spar * 2 + g, :], func=AF.Exp, scale=scale)
                                return r

                            def pv(e, tk=tk, spar=spar):
                                r = None
                                for g in range(G):
                                    kt = tk * G + g
                                    first = (kt == 0)
                                    last = (kt == KT - 1)
                                    for ec in range(EC):
                                        r = e.matmul(ps[:, 4 + ec, :], lhsT=vbuf[:, kt, ec * 128:(ec + 1) * 128], rhs=pbuf[:, spar, g, :],
                                                     start=first, stop=last)
                                    r = e.matmul(ps[:, 6, :], lhsT=ONES, rhs=pbuf[:, spar, g, :], start=first, stop=last)
                                return r

                            sc.at(s + tk, "pe", smm)
                            sc.at(s + tk + 1, "act", ex)
                            sc.at(s + tk + 2, "pe", pv)
                        s = s + ntask + 2

                        def ev(e, m=m):
                            e.reciprocal(out=rsum[:, :], in_=ps[:, 6, :])
                            r = None
                            for ec in range(EC):
                                r = e.tensor_tensor(out=ob[:, m, ec, :], in0=ps[:, 4 + ec, :], in1=rsum[:, :], op=ALU.mult)
                            return r

                        sc.at(s, "dve", ev)
                        s += 1
                    opar = qb % 2
                    if nmaps == 2:
                        def cmb(e):
                            r = None
                            for ec in range(2):
                                r = e.scalar_tensor_tensor(out=of[:, ec, :], in0=ob[:, 1, ec, :], scalar=lamc[:, diff_j * 4:diff_j * 4 + 1],
                                                           in1=ob[:, 0, ec, :], op0=ALU.mult, op1=ALU.add)
                            return r

                        def sq2(e):
                            r = None
                            for ec in range(2):
                                r = e.activation(out=osq[:, ec, :], in_=of[:, ec, :], func=AF.Square)
                            return r

                        def mm2(e):
                            e.matmul(ps[:, 7, :], lhsT=ONES, rhs=osq[:, 0, :], start=True, stop=False)
                            return e.matmul(ps[:, 7, :], lhsT=ONES, rhs=osq[:, 1, :], start=False, stop=True)

                        def rs(e):
                            return e.activation(out=orst[:, :], in_=ps[:, 7, :], func=AF.Sqrt, bias=epsc[:, 0:1], scale=1.0 / 256.0)

                        def fin(e, opar=opar):
                            e.reciprocal(out=orst[:, :], in_=orst[:, :])
                            r = None
                            for ec in range(2):
                                e.tensor_scalar(out=of[:, ec, :], in0=of[:, ec, :], scalar1=gsub_fn(ec), scalar2=(1.0 - li),
                                                op0=ALU.mult, op1=ALU.mult)
                                r = e.tensor_tensor(out=oout[:, opar, ec, :], in0=of[:, ec, :], in1=orst[:, :], op=ALU.mult)
                            return r

                        sc.at(s, "dve", cmb)
                        sc.at(s + 1, "act", sq2)
                        sc.at(s + 2, "pe", mm2)
                        sc.at(s + 3, "act", rs)
                        sc.at(s + 4, "dve", fin)
                        s += 5
                    else:
                        def fin(e, opar=opar):
                            return e.tensor_copy(out=oout[:, opar, 0, :], in_=ob[:, 0, 0, :])

                        sc.at(s, "dve", fin)
                        s += 1

                    def st_o(e, hh=hh, q0=q0, opar=opar):
                        r = []
                        for ec in range(EC):
                            om, oo = orow_fn(hh, ec)
                            r.append(e.dma_start(out=drow(ou[:, q0:q0 + 512], om, oo, 128), in_=oout[:, opar, ec, :]))
                        return r

                    sc.at(s, "sp", st_o, ndma=EC)
                    s += 1
            sc.at(s, "pool", lambda e: [e.dma_start(out=f_(), in_=s_) for (f_, s_) in cout], ndma=len(cout))
            s += 1
            sc.region(rb, s, nunits)
            return sc.next()

        epsc = sb("epsc", [128, 1], F32)
        sc.at(1, "dve", lambda e: e.memset(epsc[:, :], EPS))

        def chk(k):
            if cfg.stop == k:
                raise StopBuild()

        try:
            s = sc.next()
            chk(0)
            for L in range(cfg.depth):
                j = L // 2
                gm = lambda c, L=L: gsb[:, (0 * cfg.depth + L) * DC + c:(0 * cfg.depth + L) * DC + c + 1]
                gl = lambda c, L=L: gsb[:, (1 * cfg.depth + L) * DC + c:(1 * cfg.depth + L) * DC + c + 1]
                gp = lambda c, L=L: gsb[:, (2 * cfg.depth + L) * DC + c:(2 * cfg.depth + L) * DC + c + 1]
                if L % 2 == 0:
                    li = 0.8 - 0.6 * math.exp(-0.3 * L)
                    rb = s
                    tok0 = TOK0
                    sc.at(s, "sp", lambda e: [e.dma_start(out=Xblk[:, :], in_=dcol(outT, TB, 0, TB)),
                                              e.dma_start(out=tabblk[:, :], in_=dcol(tabd, TB, 0, TB))], ndma=2)
                    s += 1
                    s = norm_phase(s, Xblk, DC, tok0, gm, hT)
                    chk(1)
                    if cfg.stop == 10:
                        s = gemm_fm(s, d_wqk, j * 4 * HD, 2 * HD, DC, hT, TB, epi_store_f32(latd, lambda jj: 0, tok0))
                        chk(10)
                    s = gemm_fm(s, d_wqk, j * 4 * HD, 2 * HD, DC, hT, TB,
                                epi_qk(qblk, lambda jj: jj * 128, tok0, lambda jj, j=j: dg[:, j * 8 + 0:j * 8 + 1], ONES, PERMD, 0))
                    s = gemm_fm(s, d_wqk, j * 4 * HD + 2 * HD, 2 * HD, DC, hT, TB,
                                epi_qk(kblk, lambda jj: jj * 128, tok0, lambda jj, j=j: dg[:, j * 8 + 1:j * 8 + 2], ONES, PERMD, 0))
                    chk(2)
                    s = gemm_tm(s, d_wv, j * HD, HD, DC, hT, tok0, vblk, 0)
                    chk(3)
                    nqr = 2 * HD * 128
                    sc.at(s, "pool", lambda e, nqr=nqr: [e.dma_start(out=dcol(qTd[0:nqr, :], TB, 0, TB), in_=qblk[0:nqr, :]),
                                                       e.dma_start(out=dcol(kTd[0:nqr, :], TB, 0, TB), in_=kblk[0:nqr, :]),
                                                       e.dma_start(out=drow(vd, TB, 0, TB), in_=vblk[:, :])], ndma=3)
                    s += 1
                    sc.region(rb, s, NTB)
                    s = attention(s, HD, 1, 2, 256,
                                  lambda hh, m: [(0, m * 128, 128, 0)],
                                  lambda hh, m: [(0, m * 128, 128, 0)],
                                  lambda hh: (0, 0),
                                  lambda hh, ec: (0, ec * 128),
                                  128 ** -0.5,
                                  [(qu[0:256, :], lambda: drow(qTd, 256, 0, 256)),
                                   (ku[0:256, :], lambda: drow(kTd, 256, 0, 256)),
                                   (vu[:, 0:256], lambda: dcol(vd, 256, 0, 256))],
                                  [(lambda: drow(oTd, 256, 0, 256), ou[0:256, :])],
                                  diff_j=j,
                                  gsub_fn=lambda ec, j=j: dg[:, j * 8 + 2 + ec:j * 8 + 3 + ec], li=li)
                    chk(4)
                    wo, wob, okc = d_wo, j * DC, 2 * HD
                else:
                    gb = j * (QC + KVC + 4)
                    rb = s
                    tok0 = TOK0
                    sc.at(s, "sp", lambda e: [e.dma_start(out=Xblk[:, :], in_=dcol(outT, TB, 0, TB)),
                                              e.dma_start(out=tabblk[:, :], in_=dcol(tabd, TB, 0, TB))], ndma=2)
                    s += 1
                    s = norm_phase(s, Xblk, DC, tok0, gm, hT)
                    nw = QC + KVC + 1
                    s = gemm_fm(s, m_win, j * nw, QC + KVC, DC, hT, TB, epi_store_f32(latd, lambda jj: jj * 128, tok0))
                    s = gemm_fm(s, m_win, j * nw + QC + KVC, 1, DC, hT, TB,
                                epi_qk(kblk, lambda jj: HM * 128, tok0, lambda jj, gb=gb: mg[:, gb + QC + KVC + 3:gb + QC + KVC + 4],
                                       ONES64, PERMM, 1))
                    cq = aT[:, 0, :, :]
                    s = norm_phase(s, latd, QC, tok0, lambda c, gb=gb: mg[:, gb + c:gb + c + 1], cq)
                    s = gemm_fm(s, m_wuq, j * (HM + HM // 2), HM, QC, cq, TB,
                                epi_qk(qblk, lambda jj: jj * 128, tok0, lambda jj, gb=gb: mg[:, gb + QC + KVC:gb + QC + KVC + 1], ONES, None, 1))
                    s = gemm_fm(s, m_wuq, j * (HM + HM // 2) + HM, HM // 2, QC, cq, TB,
                                epi_qk(qblk, lambda jj: (HM + jj) * 128, tok0,
                                       lambda jj, gb=gb: mg[:, gb + QC + KVC + 1:gb + QC + KVC + 2], ONES64, PERMM, 1))
                    ckv = aT[:, 1, :, :]
                    s = norm_phase(s, latd[QC * 128:(QC + KVC) * 128, :], KVC, tok0,
                                   lambda c, gb=gb: mg[:, gb + QC + c:gb + QC + c + 1], ckv)
                    s = gemm_fm(s, m_wuk, j * HM, HM, KVC, ckv, TB,
                                epi_qk(kblk, lambda jj: jj * 128, tok0, lambda jj, gb=gb: mg[:, gb + QC + KVC + 2:gb + QC + KVC + 3], ONES, None, 1))
                    s = gemm_tm(s, m_wuv, j * (HM // 2), HM // 2, KVC, ckv, tok0, vblk, 0)
                    nqr = (HM + HM // 2) * 128
                    nkr = (HM + 1) * 128
                    sc.at(s, "pool", lambda e, nqr=nqr, nkr=nkr: [e.dma_start(out=dcol(qTd[0:nqr, :], TB, 0, TB), in_=qblk[0:nqr, :]),
                                                                e.dma_start(out=dcol(kTd[0:nkr, :], TB, 0, TB), in_=kblk[0:nkr, :]),
                                                                e.dma_start(out=drow(vd[:, 0:HM * 128], TB, 0, TB), in_=vblk[:, 0:HM * 128])], ndma=3)
                    s += 1
                    sc.region(rb, s, NTB)
                    s = attention(s, HM // 2, 2, 1, 128,
                                  lambda hh, m: [(0, hh * 128, 128, 0), (0, 256 + hh * 64, 64, hh * 64)],
                                  lambda hh, m: [(0, hh * 128, 128, 0), (0, 256 + hh * 64, 64, hh * 64)],
                                  lambda hh: (0, hh * 128),
                                  lambda hh, ec: (0, hh * 128),
                                  192 ** -0.5,
                                  [(qu[0:256, :], lambda: drow(qTd, 256, 0, 256)),
                                   (qu[256:384, :], lambda: drow(qTd, 128, HM * 128, 128)),
                                   (ku[0:256, :], lambda: drow(kTd, 256, 0, 256)),
                                   (ku[256:384, :], lambda: kTd[HM * 128:HM * 128 + 128, :]),
                                   (vu[:, 0:256], lambda: dcol(vd, 256, 0, 256))],
                                  [(lambda: drow(oTd, 256, 0, 256), ou[0:256, :])])
                    wo, wob, okc = m_wo, j * DC, HM
                rb = s
                tok0 = TOK0
                sc.at(s, "sp", lambda e, okc=okc, L=L: [e.dma_start(out=Xblk[:, :], in_=dcol(outT, TB, 0, TB)),
                                                        e.dma_start(out=oblk[0:okc * 128, :], in_=dcol(oTd[0:okc * 128, :], TB, 0, TB)),
                                                        e.dma_start(out=pblk[:, :], in_=dcol(pT[L * cfg.ple:(L + 1) * cfg.ple, :], TB, 0, TB))], ndma=3)
                s += 1
                def ldo(e, tok0=tok0, okc=okc):
                    return [e.dma_start(out=hT[:, 0:okc, :], in_=dcol(oblk[0:okc * 128, :], TBM, 0, TB).rearrange("(c p) t -> p c t", p=128))]

                sc.at(s, "sp", ldo, ndma=1)
                s += 1
                s = gemm_fm(s, wo, wob, DC, okc, hT, TB, epi_accum(tok0), nh_per_task=2)
                chk(5)
                s = norm_phase(s, Xblk, DC, tok0, gl, hT)
                for hb in range(NHB):
                    aset = hb % 2
                    s = gemm_fm(s, w1, L * FC + hb * HB, HB, DC, hT, TB, epi_relu2(aset))
                    s = gemm_fm(s, w2, (L * NHB + hb) * DC, DC, HB, aT[:, aset, :, :], TB, epi_accum(tok0), nh_per_task=2)
                chk(6)
                s = norm_phase(s, Xblk, DC, tok0, gp, hT)

                def ldp(e, tok0=tok0, L=L):
                    return [e.dma_start(out=pTs[:, :, :], in_=dcol(pblk[:, :], TBM, 0, TB).rearrange("(c p) t -> p c t", p=128))]

                sc.at(s, "pool", ldp, ndma=1)
                s += 1
                s = gemm_fm(s, wg, L * DC, DC, DC, hT, TB, epi_gate(tok0), extra_kc=PC, extra_in=pTs)
                chk(7)
                sc.at(s, "pool", lambda e: [e.dma_start(out=dcol(outT, TB, 0, TB), in_=Xblk[:, :])], ndma=1)
                s += 1
                sc.region(rb, s, NTB)

        except StopBuild:
            pass
        sc.emit(nc, block, sems)
    return nc


def prep_shared(cfg, inp):
    D, DC, depth = cfg.D, cfg.DC, cfg.depth
    FC = cfg.DFF // 128
    HB = cfg.HB
    NHB = FC // HB
    HD, HM = cfg.diff_heads, cfg.mla_heads
    QC, KVC = cfg.q_rank // 128, cfg.kv_rank // 128
    ND, NM = (depth + 1) // 2, depth // 2
    f = lambda a: np.asarray(a, dtype=np.float32)
    out = {}
    ct = const_tables(cfg)
    out["consts"] = np.concatenate([ct["ones"], ct["ones64"], ct["perm_d"], ct["perm_m"], ct["cols"]], axis=1)
    out["gcols"] = np.concatenate([col_layout(f(inp[k])[L], DC) for k in ("g_mix", "g_mlp", "g_ple") for L in range(depth)], axis=1)
    seq = lambda n: np.arange(n).reshape(-1, 128)
    out["w1"] = np.concatenate([tile_w(f(inp["w1"])[L], seq(cfg.DFF)) for L in range(depth)], 0)
    w2l = []
    for L in range(depth):
        W = f(inp["w2"])[L]
        for hb in range(NHB):
            w2l.append(tile_w(W[hb * HB * 128:(hb + 1) * HB * 128, :], seq(D)))
    out["w2"] = np.concatenate(w2l, 0)
    out["wg"] = np.concatenate([np.concatenate([tile_w(f(inp["w_gate"])[L], seq(D)), tile_w(f(inp["w_ple"])[L], seq(D))], axis=2)
                                for L in range(depth)], 0)
    QKW = HD * 256
    out["d_wqk"] = np.concatenate([tile_w(f(inp["diff_w_in"])[j][:, :2 * QKW], seq(2 * QKW)) for j in range(ND)], 0)
    out["d_wv"] = np.concatenate([tile_w_tm(f(inp["diff_w_in"])[j][:, 2 * QKW:]) for j in range(ND)], 0)
    out["d_wo"] = np.concatenate([tile_w(f(inp["diff_w_out"])[j], seq(D)) for j in range(ND)], 0)
    dgl = []
    for j in range(ND):
        lam = f(inp["diff_lambda"])[j]
        gs = f(inp["diff_g_sub"])[j]
        dgl.append(np.stack([f(inp["diff_g_q"])[j], f(inp["diff_g_k"])[j], gs[:128], gs[128:], lam[0], lam[1], lam[2], lam[3]], axis=1))
    out["d_g"] = np.ascontiguousarray(np.concatenate(dgl, 1))
    win_cols = np.concatenate([seq(cfg.q_rank + cfg.kv_rank),
                               (cfg.q_rank + cfg.kv_rank + np.tile(np.arange(64), 2)).reshape(1, 128)], 0)
    out["m_win"] = np.concatenate([tile_w(f(inp["mla_w_in"])[j], win_cols) for j in range(NM)], 0)
    nope_cols = np.stack([h * 192 + np.arange(128) for h in range(HM)])
    rope_cols = np.stack([np.concatenate([(2 * i) * 192 + 128 + np.arange(64), (2 * i + 1) * 192 + 128 + np.arange(64)]) for i in range(HM // 2)])
    out["m_wuq"] = np.concatenate([tile_w(f(inp["mla_w_uq"])[j], np.concatenate([nope_cols, rope_cols], 0)) for j in range(NM)], 0)
    kn_cols = np.stack([h * 256 + np.arange(128) for h in range(HM)])
    out["m_wuk"] = np.concatenate([tile_w(f(inp["mla_w_ukv"])[j], kn_cols) for j in range(NM)], 0)
    v_cols = np.concatenate([h * 256 + 128 + np.arange(128) for h in range(HM)])
    out["m_wuv"] = np.concatenate([tile_w_tm(f(inp["mla_w_ukv"])[j][:, v_cols]) for j in range(NM)], 0)
    out["m_wo"] = np.concatenate([tile_w(f(inp["mla_w_out"])[j], seq(D)) for j in range(NM)], 0)
    mgl = []
    for j in range(NM):
        gq, gk = f(inp["mla_g_q"])[j], f(inp["mla_g_k"])[j]
        mgl.append(np.concatenate([col_layout(f(inp["mla_g_cq"])[j], QC), col_layout(f(inp["mla_g_ckv"])[j], KVC),
                                   np.stack([gq[:128], np.tile(gq[128:], 2), gk[:128], np.tile(gk[128:], 2)], axis=1)], axis=1))
    out["m_g"] = np.ascontiguousarray(np.concatenate(mgl, 1))
    return out


def run(cfg, inp):
    nc = build(cfg)
    shared = prep_shared(cfg, inp)
    x = np.asarray(inp["x"], np.float32)
    p = np.asarray(inp["p"], np.float32)
    pos = np.asarray(inp["positions"]).astype(np.int32)
    B = x.shape[0]
    in_maps = []
    for b in range(B):
        m = dict(shared)
        m["xT"] = np.ascontiguousarray(x[b].T)
        m["pT"] = np.ascontiguousarray(np.concatenate([p[L, b].T for L in range(cfg.depth)], 0))
        m["posb"] = np.ascontiguousarray(np.broadcast_to(pos[b][None, :], (128, cfg.T)))
        in_maps.append(m)
    res = run_bass_kernel_spmd(nc, in_maps, core_ids=list(range(B)))
    return np.stack([np.ascontiguousarray(res.results[b]["outT"].T) for b in range(B)], 0).astype(np.float32)


def kernel(**inputs):
    return run(Cfg(), inputs)
```

```python
import math
from collections import defaultdict
from contextlib import ExitStack

import numpy as np
import concourse.bass as bass
import concourse.mybir as mybir
from concourse.bass_utils import run_bass_kernel_spmd

F32 = mybir.dt.float32
BF16 = mybir.dt.bfloat16
I32 = mybir.dt.int32
AF = mybir.ActivationFunctionType
ALU = mybir.AluOpType
EPS = 1e-6
THETA = 500000.0


class StopBuild(Exception):
    pass


class Cfg:
    stop = None
    def __init__(self, D=4096, T=4096, TB=1024, DFF=16384, depth=4, diff_heads=16,
                 mla_heads=32, q_rank=1024, kv_rank=512, ple=256, n_cores=2):
        self.D, self.T, self.TB, self.DFF, self.depth = D, T, TB, DFF, depth
        self.diff_heads, self.mla_heads = diff_heads, mla_heads
        self.q_rank, self.kv_rank, self.ple = q_rank, kv_rank, ple
        self.n_cores = n_cores
        self.DC = D // 128
        self.NTB = T // TB
        self.NH = TB // 512
        self.HB = 16 if DFF >= 2048 else 8


class LV:
    i = 0
    cache = {}


def dcol(ap, mult, off, n, fused=False):
    assert isinstance(LV.i, int)
    st = LV.i * mult + off
    return ap[:, st:st + n]


def drow(ap, mult, off, n):
    assert isinstance(LV.i, int)
    st = LV.i * mult + off
    return ap[st:st + n, :]


class _MockIns:
    def then_inc(self, *a, **k):
        return self


class CountEng:
    def __init__(self):
        self.n = 0

    def __getattr__(self, name):
        def w(*a, **k):
            self.n += 1
            return _MockIns()
        return w


class SerialEng:
    def __init__(self, eng):
        self._e = eng
        self.wv = None
        self.sem = None
        self.k = 0

    def wait_ge(self, *a, **k):
        return self._e.wait_ge(*a, **k)

    def __getattr__(self, name):
        f = getattr(self._e, name)

        def w(*a, **k):
            if self.k:
                self.wv(self.k)
            ins = f(*a, **k)
            ins.then_inc(self.sem, 1)
            self.k += 1
            return ins
        return w

    def finish_stage(self, sem):
        self.wv(self.k)
        self._e.sem_inc(sem, 1)


class Sched:
    COMPUTE = ("pe", "act", "dve")
    DMAQ = ("pool", "sp")
    SELF = ("self_act", "self_dve")

    def __init__(self):
        self.st = defaultdict(lambda: defaultdict(list))
        self.ndma = defaultdict(lambda: defaultdict(int))
        self.maxs = -1
        self.regions = {}

    def at(self, s, eng, fn, ndma=0):
        self.st[s][eng].append(fn)
        if eng in self.DMAQ:
            assert ndma > 0
            self.ndma[s][eng] += ndma
        self.maxs = max(self.maxs, s)

    def next(self):
        return self.maxs + 1

    def region(self, b, e, n):
        assert e > b
        self.regions[b] = (e, n)

    def emit(self, nc, block, sems):
        S = self.maxs + 1
        allx = self.COMPUTE + self.DMAQ + self.SELF
        nins = {"act": [0] * S, "dve": [0] * S}
        LV.i = 0
        for e in ("act", "dve"):
            for s in range(S):
                fns = self.st[s].get(e)
                if fns:
                    ce = CountEng()
                    for fn in fns:
                        fn(ce)
                    nins[e][s] = ce.n
        inc = {x: [0] * S for x in allx}
        for s in range(S):
            for e in self.COMPUTE:
                inc[e][s] = 1 if self.st[s].get(e) else 0
            for e in self.DMAQ:
                inc[e][s] = 16 * self.ndma[s].get(e, 0)
            inc["self_act"][s] = nins["act"][s]
            inc["self_dve"][s] = nins["dve"][s]
        reg_of = [None] * S
        delta = {}
        for b, (e_, n) in self.regions.items():
            delta[b] = {x: sum(inc[x][b:e_]) for x in allx}
            for s in range(b, e_):
                assert reg_of[s] is None
                reg_of[s] = b
        cum = {x: [0] * (S + 1) for x in allx}
        for s in range(S):
            for x in allx:
                cum[x][s + 1] = cum[x][s] + inc[x][s]
            b = reg_of[s]
            if b is not None and s + 1 == self.regions[b][0]:
                n = self.regions[b][1]
                for x in allx:
                    cum[x][s + 1] += (n - 1) * delta[b][x]
        names = {"pe": "tensor", "act": "scalar", "dve": "vector", "pool": "gpsimd", "sp": "sync"}

        def make(e):
            def body(raw):
                eng = SerialEng(raw) if e in ("act", "dve") else raw
                selfx = "self_" + e
                if e in ("act", "dve"):
                    eng.sem = sems[selfx]
                waited = {x: -1 for x in self.COMPUTE + self.DMAQ}
                st8 = {"scr": None, "i": None, "it": None}

                def wait(x, v, b):
                    if b is not None and delta[b][x]:
                        if st8["it"] is not None:
                            raw.wait_ge(sems[x], v + st8["it"] * delta[b][x])
                        else:
                            raw.reg_add(st8["scr"], st8["base"][x], v)
                            raw.wait_ge(sems[x], st8["scr"])
                    else:
                        raw.wait_ge(sems[x], v)

                def emit_stage(s, b):
                    fns = self.st[s].get(e)
                    if not fns:
                        return
                    for x in self.COMPUTE + self.DMAQ:
                        if x == e and e in self.COMPUTE:
                            continue
                        tgt = cum[x][s]
                        if tgt > waited[x]:
                            wait(x, tgt, b)
                            waited[x] = tgt
                    if e in ("act", "dve"):
                        eng.wv = lambda k, s=s, b=b: wait(selfx, cum[selfx][s] + k, b)
                        eng.k = 0
                    last = None
                    cnt = 0
                    for fn in fns:
                        r = fn(eng)
                        if e in self.DMAQ:
                            for ins in r:
                                ins.then_inc(sems[e], 16)
                                cnt += 1
                        else:
                            last = r
                    if e in ("act", "dve"):
                        assert eng.k == nins[e][s], (e, s, eng.k, nins[e][s])
                        eng.finish_stage(sems[e])
                    elif e in self.COMPUTE:
                        last.then_inc(sems[e], 1)
                    else:
                        assert cnt == self.ndma[s][e], (s, e, cnt, self.ndma[s][e])

                s = 0
                while s < S:
                    if s in self.regions:
                        e_, n = self.regions[s]
                        if any(self.st[ss].get(e) for ss in range(s, e_)):
                            for x in waited:
                                waited[x] = -1
                            if e in self.DMAQ:
                                for it in range(n):
                                    LV.i = it
                                    st8["it"] = it
                                    for x in waited:
                                        waited[x] = -1
                                    for ss in range(s, e_):
                                        emit_stage(ss, s)
                                st8["it"] = None
                                LV.i = 0
                            else:
                                if st8["scr"] is None:
                                    st8["scr"] = raw.alloc_register("ls_%s" % e)
                                    st8["base"] = {x: raw.alloc_register("lb_%s_%s" % (e, x)) for x in allx
                                                   if x in self.COMPUTE + self.DMAQ or x == selfx}
                                xs = [x for x in st8["base"] if delta[s][x]]
                                for x in xs:
                                    raw.reg_mov(st8["base"][x], 0)
                                with raw.Fori(0, n) as i:
                                    LV.i = i
                                    st8["i"] = i
                                    for ss in range(s, e_):
                                        emit_stage(ss, s)
                                    for x in xs:
                                        raw.reg_add(st8["base"][x], st8["base"][x], delta[s][x])
                                LV.i = 0
                            for x in waited:
                                waited[x] = -1
                        s = e_
                    else:
                        emit_stage(s, None)
                        s += 1
                for x in self.COMPUTE + self.DMAQ:
                    if x == e and e in self.COMPUTE:
                        continue
                    raw.wait_ge(sems[x], cum[x][S])
            return body

        for e in self.COMPUTE + self.DMAQ:
            getattr(block, names[e])(make(e))


def tile_w(W, cols):
    K = W.shape[0]
    KC = K // 128
    cols = np.asarray(cols)
    nch = cols.shape[0]
    Wg = W[:, cols.reshape(-1)].reshape(KC, 128, nch, 128)
    return np.ascontiguousarray(Wg.transpose(2, 1, 0, 3))


def tile_w_tm(W):
    K, N = W.shape
    KC = K // 128
    return np.ascontiguousarray(W.reshape(KC, 128, N // 128, 128).transpose(2, 1, 0, 3))


def col_layout(g, n):
    return np.ascontiguousarray(np.asarray(g).reshape(n, 128).T)


def const_tables(cfg):
    c = {}
    c["ones"] = np.ones((128, 128), np.float32)
    blk = np.zeros((128, 128), np.float32)
    blk[:64, :64] = 1
    blk[64:, 64:] = 1
    c["ones64"] = blk
    pd = np.zeros((128, 128), np.float32)
    for m in range(16):
        pd[m + 16, m] = 1
        pd[m, m + 16] = 1
    c["perm_d"] = pd
    pm = np.zeros((128, 128), np.float32)
    for b in (0, 64):
        for m in range(32):
            pm[b + m + 32, b + m] = 1
            pm[b + m, b + m + 32] = 1
    c["perm_m"] = pm
    col = np.zeros((128, 8), np.float32)
    inv_d = (THETA ** (-np.arange(0, 32, 2, dtype=np.float32) / np.float32(32))).astype(np.float32)
    inv_m = (THETA ** (-np.arange(0, 64, 2, dtype=np.float32) / np.float32(64))).astype(np.float32)
    for p in range(128):
        if p < 32:
            col[p, 0] = inv_d[p % 16]
            col[p, 1] = 1.0
            col[p, 2] = 0.0
            col[p, 3] = -1.0 if p < 16 else 1.0
        else:
            col[p, 1] = 0.0
            col[p, 2] = 1.0
            col[p, 3] = 0.0
        q = p % 64
        col[p, 4] = inv_m[q % 32]
        col[p, 5] = 1.0
        col[p, 6] = 0.0
        col[p, 7] = -1.0 if q < 32 else 1.0
    c["cols"] = col
    return c


def build(cfg):
    nc = bass.Bass("TRN2", target_bir_lowering=False)
    D, T, TB, DC, NTB, NH, DFF = cfg.D, cfg.T, cfg.TB, cfg.DC, cfg.NTB, cfg.NH, cfg.DFF
    FC = DFF // 128
    HB = cfg.HB
    NHB = FC // HB
    ND = (cfg.depth + 1) // 2
    NM = cfg.depth // 2
    HD = cfg.diff_heads
    HM = cfg.mla_heads
    QC = cfg.q_rank // 128
    KVC = cfg.kv_rank // 128
    PC = cfg.ple // 128
    KT = T // 128
    NQB = T // 512

    def din(name, shape, dt=F32):
        return nc.dram_tensor(name, list(shape), dt, kind="ExternalInput").ap()

    def dscr(name, shape, dt):
        if getattr(cfg, "dump", False):
            return nc.dram_tensor(name, list(shape), dt, kind="ExternalOutput").ap()
        return nc.dram_tensor(name, list(shape), dt).ap()

    xT = din("xT", [D, T])
    pT = din("pT", [cfg.depth * cfg.ple, T])
    posb = din("posb", [128, T], I32)
    gcols = din("gcols", [128, 3 * cfg.depth * DC])
    consts = din("consts", [128, 4 * 128 + 8])
    w1 = din("w1", [cfg.depth * FC, 128, DC, 128])
    w2 = din("w2", [cfg.depth * NHB * DC, 128, HB, 128])
    wg = din("wg", [cfg.depth * DC, 128, DC + PC, 128])
    d_wqk = din("d_wqk", [ND * 4 * HD, 128, DC, 128])
    d_wv = din("d_wv", [ND * 2 * HD, 128, DC, 128])
    d_wo = din("d_wo", [ND * DC, 128, 2 * HD, 128])
    d_g = din("d_g", [128, ND * 8])
    m_win = din("m_win", [NM * (QC + KVC + 1), 128, DC, 128])
    m_wuq = din("m_wuq", [NM * (HM + HM // 2), 128, QC, 128])
    m_wuk = din("m_wuk", [NM * HM, 128, KVC, 128])
    m_wuv = din("m_wuv", [NM * HM, 128, KVC, 128])
    m_wo = din("m_wo", [NM * DC, 128, HM, 128])
    m_g = din("m_g", [128, NM * (QC + KVC + 4)])
    outT = nc.dram_tensor("outT", [D, T], F32, kind="ExternalOutput").ap()

    QW = max(2 * HD, HM + HM // 2) * 128
    KW = max(2 * HD, HM + 1) * 128
    qTd = dscr("qTd", [QW, T], BF16)
    kTd = dscr("kTd", [KW, T], BF16)
    vd = dscr("vd", [T, D], BF16)
    oTd = dscr("oTd", [D, T], BF16)
    latd = dscr("latd", [(QC + KVC) * 128, TB], F32)
    tabd = dscr("tabd", [4 * 128, T], F32)
    Xblk = dscr("Xblk", [D, TB], F32)
    qblk = dscr("qblk", [QW, TB], BF16)
    kblk = dscr("kblk", [KW, TB], BF16)
    vblk = dscr("vblk", [TB, D], BF16)
    tabblk = dscr("tabblk", [4 * 128, TB], F32)
    oblk = dscr("oblk", [D, TB], BF16)
    pblk = dscr("pblk", [cfg.ple, TB], F32)
    qu = dscr("qu", [3 * 128, T], BF16)
    ku = dscr("ku", [3 * 128, T], BF16)
    vu = dscr("vu", [T, 256], BF16)
    ou = dscr("ou", [256, T], BF16)

    es = ExitStack()
    with es:
        def sb(name, shape, dt):
            return es.enter_context(nc.sbuf_tensor(name, list(shape), dt))

        cst = sb("cst", [128, 4 * 128 + 8], F32)
        cstb = sb("cstb", [128, 4 * 128], BF16)
        gsb = sb("gsb", [128, 3 * cfg.depth * DC], F32)
        dg = sb("dg", [128, ND * 8], F32)
        mg = sb("mg", [128, NM * (QC + KVC + 4)], F32)
        lamc = sb("lamc", [128, ND * 4], F32)
        arena = sb("arena", [128, 32768], BF16)
        arenb = sb("arenb", [128, 8192], BF16)

        def view(ar, off, shape, dt):
            n = 1
            for d in shape:
                n *= d
            esz = 4 if dt in (F32, I32) else 2
            a = ar[:, off // 2: off // 2 + n * esz // 2]
            if dt != BF16:
                a = a.bitcast(dt)
            if len(shape) == 2:
                a = a.rearrange("p (a b) -> p a b", b=shape[1])
            elif len(shape) == 3:
                a = a.rearrange("p (a b c) -> p a b c", b=shape[1], c=shape[2])
            return a, off + n * esz

        hT, _ = view(arena, 0, [DC, TB], BF16)
        o = 0
        kbuf, o = view(arena, o, [2, T], BF16)
        vbuf, o = view(arena, o, [KT, 256], BF16)
        qbuf, o = view(arena, o, [2, 2, 512], BF16)
        pbuf, o = view(arena, o, [2, 2, 512], BF16)
        osq, o = view(arena, o, [2, 512], BF16)
        oout, o = view(arena, o, [2, 2, 512], BF16)
        ob, o = view(arena, o, [2, 2, 512], F32)
        rsum, o = view(arena, o, [512], F32)
        of, o = view(arena, o, [2, 512], F32)
        orst, o = view(arena, o, [512], F32)
        assert o <= 65536, o
        o = 0
        posi, o = view(arena, o, [512], I32)
        tmpA, o = view(arena, o, [2, 512], F32)
        tmpB, o = view(arena, o, [2, 512], F32)
        tmpC, o = view(arena, o, [512], F32)
        tmpI, o = view(arena, o, [512], I32)
        o = 0
        xbuf, o = view(arenb, o, [2, TB], F32)
        sqb, o = view(arenb, o, [2, TB], BF16)
        rstd, o = view(arenb, o, [TB], F32)
        assert o <= 16384, o
        o = 0
        ep_f, o = view(arenb, o, [2, 1024], F32)
        ep_g, o = view(arenb, o, [2, 512], F32)
        ep_r, o = view(arenb, o, [2, 512], F32)
        assert o <= 16384, o
        wbuf = sb("wbuf", [128, 2, 4608], BF16)
        ep_s = sb("ep_s", [128, 2, 512], BF16)
        ep_z = sb("ep_z", [128, 2, 512], BF16)
        ep_o = sb("ep_o", [128, 2, 1024], BF16)
        tbuf = sb("tbuf", [128, 2, 2, 512], F32)
        ep_g3 = sb("ep_g3", [128, 3, 512], F32)
        aT = sb("aT", [128, 2, HB, TB], BF16)
        pTs = sb("pTs", [128, PC, TB], BF16)

        ps = es.enter_context(nc.psum_tensor("ps", [128, 8, 512], F32))

        sems = {e: es.enter_context(nc.semaphore("sem_" + e)) for e in Sched.COMPUTE + Sched.DMAQ}
        sems["self_act"] = es.enter_context(nc.semaphore("sem_self_act"))
        sems["self_dve"] = es.enter_context(nc.semaphore("sem_self_dve"))
        block = es.enter_context(nc.Block())
        sc = Sched()

        ONES, ONES64, PERMD, PERMM = (cstb[:, i * 128:(i + 1) * 128] for i in range(4))
        CC = lambda i: cst[:, 512 + i:512 + i + 1]

        s = 0
        sc.at(s, "sp", lambda e: [e.dma_start(out=cst[:, :], in_=consts[:, :]),
                                  e.dma_start(out=gsb[:, :], in_=gcols[:, :]),
                                  e.dma_start(out=dg[:, :], in_=d_g[:, :]),
                                  e.dma_start(out=mg[:, :], in_=m_g[:, :]),
                                  e.dma_start(out=outT[:, :], in_=xT[:, :])], ndma=5)
        s += 1
        sc.at(s, "dve", lambda e: e.tensor_copy(out=cstb[:, :], in_=cst[:, 0:512]))
        for j in range(ND):
            li = 0.8 - 0.6 * math.exp(-0.3 * (2 * j))
            b = j * 8

            def f_l1(e, b=b, j=j):
                e.tensor_tensor(out=tmpA[:, 0, 0:1], in0=dg[:, b + 4:b + 5], in1=dg[:, b + 5:b + 6], op=ALU.mult)
                return e.tensor_tensor(out=tmpA[:, 0, 1:2], in0=dg[:, b + 6:b + 7], in1=dg[:, b + 7:b + 8], op=ALU.mult)

            def f_l2(e):
                return e.matmul(ps[:, 7, 0:2], lhsT=cst[:, 0:128], rhs=tmpA[:, 0, 0:2], start=True, stop=True)

            def f_l3(e):
                return e.activation(out=tmpA[:, 1, 0:2], in_=ps[:, 7, 0:2], func=AF.Exp)

            def f_l4(e, j=j, li=li):
                e.tensor_tensor(out=tmpA[:, 1, 2:3], in0=tmpA[:, 1, 1:2], in1=tmpA[:, 1, 0:1], op=ALU.subtract)
                return e.tensor_single_scalar(out=lamc[:, j * 4:j * 4 + 1], in_=tmpA[:, 1, 2:3], scalar=-li, op=ALU.add)

            sc.at(s + 1, "dve", f_l1)
            sc.at(s + 2, "pe", f_l2)
            sc.at(s + 3, "act", f_l3)
            sc.at(s + 4, "dve", f_l4)
            s += 4
        s = sc.next()
        for cb in range(T // 512):
            sl = slice(cb * 512, (cb + 1) * 512)
            sc.at(s, "sp", lambda e, sl=sl: [e.dma_start(out=posi[:, :], in_=posb[:, sl])], ndma=1)
            for ti, base in ((0, 0), (1, 4)):

                def f_t1(e, base=base):
                    TWO_PI = 2 * math.pi
                    e.tensor_copy(out=tmpA[:, 0, :], in_=posi[:, :])
                    e.tensor_scalar(out=tmpA[:, 0, :], in0=tmpA[:, 0, :], scalar1=CC(base), scalar2=None, op0=ALU.mult)
                    r = None
                    for dsti, add in ((1, math.pi / 2), (0, 0.0)):
                        dst = tmpA[:, dsti, :]
                        e.tensor_single_scalar(out=dst, in_=tmpA[:, 0, :], scalar=add, op=ALU.add)
                        e.tensor_single_scalar(out=tmpC[:, :], in_=dst, scalar=1.0 / TWO_PI, op=ALU.mult)
                        e.tensor_copy(out=tmpI[:, :], in_=tmpC[:, :])
                        e.tensor_copy(out=tmpC[:, :], in_=tmpI[:, :])
                        e.tensor_single_scalar(out=tmpC[:, :], in_=tmpC[:, :], scalar=-TWO_PI, op=ALU.mult)
                        e.tensor_tensor(out=dst, in0=dst, in1=tmpC[:, :], op=ALU.add)
                        e.tensor_single_scalar(out=tmpC[:, :], in_=dst, scalar=math.pi, op=ALU.is_gt)
                        e.tensor_single_scalar(out=tmpC[:, :], in_=tmpC[:, :], scalar=-TWO_PI, op=ALU.mult)
                        r = e.tensor_tensor(out=dst, in0=dst, in1=tmpC[:, :], op=ALU.add)
                    return r

                def f_t2(e):
                    e.activation(out=tmpA[:, 0, :], in_=tmpA[:, 0, :], func=AF.Sin)
                    return e.activation(out=tmpA[:, 1, :], in_=tmpA[:, 1, :], func=AF.Sin)

                def f_t3(e, base=base):
                    e.tensor_scalar(out=tmpB[:, 0, :], in0=tmpA[:, 1, :], scalar1=CC(base + 1), scalar2=CC(base + 2),
                                    op0=ALU.mult, op1=ALU.add)
                    return e.tensor_scalar(out=tmpB[:, 1, :], in0=tmpA[:, 0, :], scalar1=CC(base + 3), scalar2=None,
                                           op0=ALU.mult)

                def f_t4(e, ti=ti, sl=sl):
                    return [e.dma_start(out=tabd[(2 * ti) * 128:(2 * ti + 1) * 128, sl], in_=tmpB[:, 0, :]),
                            e.dma_start(out=tabd[(2 * ti + 1) * 128:(2 * ti + 2) * 128, sl], in_=tmpB[:, 1, :])]

                o = 1 + 3 * ti
                sc.at(s + o, "dve", f_t1)
                sc.at(s + o + 1, "act", f_t2)
                sc.at(s + o + 2, "dve", f_t3)
                sc.at(s + o + 3, "sp", f_t4, ndma=2)
            s += 8

        class _TokOff:
            @property
            def v(self):
                return LV.i * TB

        TOK0 = _TokOff()
        TBM = 0

        def norm_phase(s0, src, nchunk, tok0, gcol_fn, dst):
            nfeat = nchunk * 128
            s = s0
            for c in range(nchunk):
                par = c % 2

                def ld(e, c=c, par=par):
                    return [e.dma_start(out=xbuf[:, par, :], in_=dcol(src[c * 128:(c + 1) * 128, :], TBM, 0, TB))]

                def sq(e, par=par):
                    return e.activation(out=sqb[:, par, :], in_=xbuf[:, par, :], func=AF.Square)

                def mm(e, c=c, par=par):
                    r = None
                    for h in range(NH):
                        r = e.matmul(ps[:, 4 + h, :], lhsT=ONES, rhs=sqb[:, par, h * 512:(h + 1) * 512],
                                     start=(c == 0), stop=(c == nchunk - 1))
                    return r

                sc.at(s + c, "sp", ld, ndma=1)
                sc.at(s + c + 1, "act", sq)
                sc.at(s + c + 2, "pe", mm)
            s = s + nchunk + 2

            def rs1(e):
                r = None
                for h in range(NH):
                    r = e.activation(out=rstd[:, h * 512:(h + 1) * 512], in_=ps[:, 4 + h, :], func=AF.Sqrt,
                                     bias=CC(8 - 8) if False else epsc[:, 0:1], scale=1.0 / nfeat)
                return r

            def rs2(e):
                return e.reciprocal(out=rstd[:, :], in_=rstd[:, :])

            sc.at(s, "act", rs1)
            sc.at(s + 1, "dve", rs2)
            for c in range(nchunk):
                par = c % 2

                def ld(e, c=c, par=par):
                    return [e.dma_start(out=xbuf[:, par, :], in_=dcol(src[c * 128:(c + 1) * 128, :], TBM, 0, TB))]

                def scl(e, c=c, par=par):
                    return e.scalar_tensor_tensor(out=dst[:, c, :], in0=xbuf[:, par, :], scalar=gcol_fn(c),
                                                  in1=rstd[:, :], op0=ALU.mult, op1=ALU.mult)

                sc.at(s + 1 + c, "sp", ld, ndma=1)
                sc.at(s + 2 + c, "dve", scl)
            return sc.next()

        def gemm_fm(s0, wd, wbase, njobs, KC, inT, tokw, epi, nh_per_task=1, extra_kc=0, extra_in=None, psb=0):
            ntask_per_job = tokw // (512 * nh_per_task)
            KCT = KC + extra_kc
            s = s0
            ti = 0
            for j in range(njobs):
                wpar = j % 2
                wv = wbuf[:, wpar, 0:KCT * 128]

                def ldw(e, j=j, wv=wv):
                    return [e.dma_start(out=wv, in_=wd[wbase + j].rearrange("p k m -> p (k m)"))]

                sc.at(s + j * ntask_per_job, "pool", ldw, ndma=1)
                for tt in range(ntask_per_job):
                    st = s + j * ntask_per_job + tt + 1
                    ppar = ti % 2
                    banks = [psb + ppar * nh_per_task + h for h in range(nh_per_task)]
                    t0 = tt * 512 * nh_per_task

                    def mm(e, wpar=wpar, banks=banks, t0=t0, ppar=ppar):
                        r = None
                        for h, bk in enumerate(banks):
                            for kc in range(KC):
                                r = e.matmul(ps[:, bk, :], lhsT=wbuf[:, wpar, kc * 128:(kc + 1) * 128],
                                             rhs=inT[:, kc, t0 + h * 512:t0 + (h + 1) * 512],
                                             start=(kc == 0), stop=(kc == KC - 1))
                        if extra_kc:
                            for kc in range(extra_kc):
                                r = e.matmul(ps[:, 4 + ppar, :], lhsT=wbuf[:, wpar, (KC + kc) * 128:(KC + kc + 1) * 128],
                                             rhs=extra_in[:, kc, t0:t0 + 512], start=(kc == 0), stop=(kc == extra_kc - 1))
                        return r

                    if getattr(cfg, "dbg", 0) != 1:
                        sc.at(st, "pe", mm)
                    if getattr(cfg, "dbg", 0) not in (1, 2):
                        epi(st + 1, j, t0, 512 * nh_per_task, [ps[:, bk, :] for bk in banks], ti)
                    ti += 1
            return sc.next()

        def epi_qk(dst, row_fn, tok0, gcol_fn, ones_m, perm_m, tab_i):
            def epi(st, j, t0, ntok, pl, ti):
                par = ti % 2
                g3 = ti % 3
                p = pl[0]

                def e1a(e):
                    e.activation(out=ep_s[:, par, :], in_=p, func=AF.Square)
                    r = e.activation(out=ep_g3[:, g3, :], in_=p, func=AF.Identity, scale=gcol_fn(j))
                    if perm_m is not None:
                        r = e.activation(out=ep_z[:, par, :], in_=p, func=AF.Identity, scale=gcol_fn(j))
                    return r

                def e2(e):
                    r = e.matmul(ps[:, 4 + par, :], lhsT=ones_m, rhs=ep_s[:, par, :], start=True, stop=True)
                    if perm_m is not None:
                        r = e.matmul(ps[:, 6 + par, :], lhsT=perm_m, rhs=ep_z[:, par, :], start=True, stop=True)
                    return r

                def e3a(e):
                    return e.activation(out=ep_r[:, par, :], in_=ps[:, 4 + par, :], func=AF.Sqrt,
                                        bias=epsc[:, 0:1], scale=1.0 / (128.0 if ones_m is ONES else 64.0))

                def e3d(e):
                    if perm_m is None:
                        return e.tensor_copy(out=ep_f[:, par, 0:512], in_=ep_g3[:, g3, :])
                    e.tensor_tensor(out=ep_f[:, par, 0:512], in0=ps[:, 6 + par, :], in1=tbuf[:, par, 1, :], op=ALU.mult)
                    e.tensor_tensor(out=ep_f[:, par, 512:1024], in0=ep_g3[:, g3, :], in1=tbuf[:, par, 0, :], op=ALU.mult)
                    return e.tensor_tensor(out=ep_f[:, par, 0:512], in0=ep_f[:, par, 0:512], in1=ep_f[:, par, 512:1024], op=ALU.add)

                def e4(e):
                    e.reciprocal(out=ep_r[:, par, :], in_=ep_r[:, par, :])
                    return e.tensor_tensor(out=ep_o[:, par, 0:512], in0=ep_f[:, par, 0:512], in1=ep_r[:, par, :], op=ALU.mult)

                def e5(e):
                    r0 = row_fn(j)
                    return [e.dma_start(out=dcol(dst[r0:r0 + 128, :], TBM, t0, 512), in_=ep_o[:, par, 0:512])]

                def eld(e):
                    return [e.dma_start(out=tbuf[:, par, 0, :], in_=dcol(tabblk[(2 * tab_i) * 128:(2 * tab_i + 1) * 128, :], TBM, t0, 512)),
                            e.dma_start(out=tbuf[:, par, 1, :], in_=dcol(tabblk[(2 * tab_i + 1) * 128:(2 * tab_i + 2) * 128, :], TBM, t0, 512))]

                skip = getattr(cfg, "skip", ())
                if "e1a" not in skip:
                    sc.at(st, "act", e1a)
                if perm_m is not None and "eld" not in skip:
                    sc.at(st + 1, "sp", eld, ndma=2)
                if "e2" not in skip:
                    sc.at(st + 1, "pe", e2)
                if "e3a" not in skip:
                    sc.at(st + 2, "act", e3a)
                if "e3d" not in skip:
                    sc.at(st + 2, "dve", e3d)
                if "e4" not in skip:
                    sc.at(st + 3, "dve", e4)
                if "e5" not in skip:
                    sc.at(st + 4, "sp", e5, ndma=1)
            return epi

        def epi_store_f32(dst, row_fn, tok0):
            def epi(st, j, t0, ntok, pl, ti):
                par = ti % 2

                def c(e):
                    return e.activation(out=ep_f[:, par, 0:512], in_=pl[0], func=AF.Identity)

                def d(e):
                    r0 = row_fn(j)
                    return [e.dma_start(out=dcol(dst[r0:r0 + 128, :], TBM, t0, 512), in_=ep_f[:, par, 0:512])]

                dbg = getattr(cfg, "dbg", 0)
                if dbg == 6:
                    sc.at(st, "dve", lambda e: e.tensor_copy(out=ep_f[:, par, 0:512], in_=pl[0]))
                elif dbg != 4:
                    sc.at(st, "act", c)
                if dbg != 3:
                    sc.at(st + (2 if dbg == 5 else 1), "sp", d, ndma=1)
            return epi

        def epi_accum(tok0):
            def epi(st, j, t0, ntok, pl, ti):
                par = ti % 2

                def c(e):
                    r = None
                    for h, p in enumerate(pl):
                        r = e.activation(out=ep_f[:, par, h * 512:(h + 1) * 512], in_=p, func=AF.Identity)
                    return r

                def d(e):
                    return [e.dma_start(out=dcol(Xblk[j * 128:(j + 1) * 128, :], TBM, t0, ntok, fused=True), in_=ep_f[:, par, 0:ntok],
                                        accum_op=ALU.add)]

                sc.at(st, "act", c)
                sc.at(st + 1, "pool", d, ndma=1)
            return epi

        def epi_relu2(aset):
            def epi(st, j, t0, ntok, pl, ti):
                par = ti % 2

                def c(e):
                    return e.activation(out=ep_f[:, par, 0:512], in_=pl[0], func=AF.Relu)

                def d(e):
                    return e.tensor_tensor(out=aT[:, aset, j, t0:t0 + 512], in0=ep_f[:, par, 0:512], in1=ep_f[:, par, 0:512], op=ALU.mult)

                sc.at(st, "act", c)
                sc.at(st + 1, "dve", d)
            return epi

        def epi_gate(tok0):
            def epi(st, j, t0, ntok, pl, ti):
                par = ti % 2

                def c(e):
                    return e.activation(out=ep_g[:, par, :], in_=pl[0], func=AF.Sigmoid)

                def c2(e):
                    return e.tensor_copy(out=ep_r[:, par, :], in_=ps[:, 4 + par, :])

                def m(e):
                    return e.tensor_tensor(out=ep_f[:, par, 0:512], in0=ep_g[:, par, :], in1=ep_r[:, par, :], op=ALU.mult)

                def d(e):
                    return [e.dma_start(out=dcol(Xblk[j * 128:(j + 1) * 128, :], TBM, t0, 512, fused=True), in_=ep_f[:, par, 0:512], accum_op=ALU.add)]

                sc.at(st, "act", c)
                sc.at(st, "dve", c2)
                sc.at(st + 1, "dve", m)
                sc.at(st + 2, "pool", d, ndma=1)
            return epi

        def gemm_tm(s0, wd, wbase, ncb, KC, inT, tok0, dst, col0):
            s = s0
            ntt = TB // 128
            ti = 0
            for cb in range(ncb):
                wpar = cb % 2

                def ldw(e, cb=cb, wpar=wpar):
                    return [e.dma_start(out=wbuf[:, wpar, 0:KC * 128], in_=wd[wbase + cb].rearrange("p k m -> p (k m)"))]

                sc.at(s + cb * ntt, "pool", ldw, ndma=1)
                for tt in range(ntt):
                    st = s + cb * ntt + tt + 1
                    par = ti % 2

                    def mm(e, wpar=wpar, tt=tt, par=par):
                        r = None
                        for kc in range(KC):
                            r = e.matmul(ps[:, par, 0:128], lhsT=inT[:, kc, tt * 128:(tt + 1) * 128],
                                         rhs=wbuf[:, wpar, kc * 128:(kc + 1) * 128], start=(kc == 0), stop=(kc == KC - 1))
                        return r

                    def c(e, par=par):
                        return e.activation(out=ep_o[:, par, 0:128], in_=ps[:, par, 0:128], func=AF.Identity)

                    def d(e, par=par, tt=tt, cb=cb):
                        return [e.dma_start(out=drow(dst[:, col0 + cb * 128:col0 + (cb + 1) * 128], TBM, tt * 128, 128), in_=ep_o[:, par, 0:128])]

                    sc.at(st, "pe", mm)
                    sc.at(st + 1, "act", c)
                    sc.at(st + 2, "sp", d, ndma=1)
                    ti += 1
            return sc.next()

        def attention(s0, nunits, hpu, nmaps, vw, qrow_fn, krow_fn, vcol_fn, orow_fn, scale, cin, cout, diff_j=None, gsub_fn=None, li=0.0):
            EC = vw // 128
            G = 2
            s = s0
            rb = s
            sc.at(s, "sp", lambda e: [e.dma_start(out=d_, in_=f_()) for (d_, f_) in cin], ndma=len(cin))
            s += 1
            for hh in range(hpu):
                def ldk(e, hh=hh):
                    r = []
                    for m in range(nmaps):
                        for pi, (mult, off, nr, p0) in enumerate(krow_fn(hh, m)):
                            idx = m if nmaps == 2 else pi
                            r.append(e.dma_start(out=kbuf[p0:p0 + nr, idx, :], in_=drow(ku, mult, off, nr)))
                    vm, vo = vcol_fn(hh)
                    r.append(e.dma_start(out=vbuf[:, :, 0:vw], in_=dcol(vu, vm, vo, vw).rearrange("(kt p) e -> p kt e", p=128)))
                    return r

                nk = sum(len(krow_fn(hh, m)) for m in range(nmaps)) + 1
                sc.at(s, "sp", ldk, ndma=nk)
                s += 1
                for qb in range(NQB):
                    qpar = qb % 2
                    q0 = qb * 512

                    def ldq(e, hh=hh, qpar=qpar, q0=q0):
                        r = []
                        for m in range(nmaps):
                            for pi, (mult, off, nr, p0) in enumerate(qrow_fn(hh, m)):
                                idx = m if nmaps == 2 else pi
                                r.append(e.dma_start(out=qbuf[p0:p0 + nr, qpar, idx, :], in_=drow(qu[:, q0:q0 + 512], mult, off, nr)))
                        return r

                    nq = sum(len(qrow_fn(hh, m)) for m in range(nmaps))
                    sc.at(s, "sp", ldq, ndma=nq)
                    s += 1
                    for m in range(nmaps):
                        parts_q = [(nr, p0) for (_, _, nr, p0) in qrow_fn(hh, m)]
                        ntask = KT // G
                        for tk in range(ntask):
                            spar = tk % 2

                            def smm(e, tk=tk, spar=spar, m=m, parts_q=parts_q, qpar=qpar):
                                r = None
                                for g in range(G):
                                    kt = tk * G + g
                                    for pi, (nr, p0) in enumerate(parts_q):
                                        idx = m if nmaps == 2 else pi
                                        r = e.matmul(ps[:, spar * 2 + g, :], lhsT=kbuf[p0:p0 + nr, idx, kt * 128:(kt + 1) * 128],
                                                     rhs=qbuf[p0:p0 + nr, qpar, idx, :], start=(pi == 0), stop=(pi == len(parts_q) - 1))
                                return r

                            def ex(e, spar=spar):
                                r = None
                                for g in range(G):
                                    r = e.activation(out=pbuf[:, spar, g, :], in_=ps[:, spar * 2 + g, :], func=AF.Exp, scale=scale)
                                return r

                            def pv(e, tk=tk, spar=spar):
                                r = None
                                for g in range(G):
                                    kt = tk * G + g
                                    first = (kt == 0)
                                    last = (kt == KT - 1)
                                    for ec in range(EC):
                                        r = e.matmul(ps[:, 4 + ec, :], lhsT=vbuf[:, kt, ec * 128:(ec + 1) * 128], rhs=pbuf[:, spar, g, :],
                                                     start=first, stop=last)
                                    r = e.matmul(ps[:, 6, :], lhsT=ONES, rhs=pbuf[:, spar, g, :], start=first, stop=last)
                                return r

                            sc.at(s + tk, "pe", smm)
                            sc.at(s + tk + 1, "act", ex)
                            sc.at(s + tk + 2, "pe", pv)
                        s = s + ntask + 2

                        def ev(e, m=m):
                            e.reciprocal(out=rsum[:, :], in_=ps[:, 6, :])
                            r = None
                            for ec in range(EC):
                                r = e.tensor_tensor(out=ob[:, m, ec, :], in0=ps[:, 4 + ec, :], in1=rsum[:, :], op=ALU.mult)
                            return r

                        sc.at(s, "dve", ev)
                        s += 1
                    opar = qb % 2
                    if nmaps == 2:
                        def cmb(e):
                            r = None
                            for ec in range(2):
                                r = e.scalar_tensor_tensor(out=of[:, ec, :], in0=ob[:, 1, ec, :], scalar=lamc[:, diff_j * 4:diff_j * 4 + 1],
                                                           in1=ob[:, 0, ec, :], op0=ALU.mult, op1=ALU.add)
                            return r

                        def sq2(e):
                            r = None
                            for ec in range(2):
                                r = e.activation(out=osq[:, ec, :], in_=of[:, ec, :], func=AF.Square)
                            return r

                        def mm2(e):
                            e.matmul(ps[:, 7, :], lhsT=ONES, rhs=osq[:, 0, :], start=True, stop=False)
                            return e.matmul(ps[:, 7, :], lhsT=ONES, rhs=osq[:, 1, :], start=False, stop=True)

                        def rs(e):
                            return e.activation(out=orst[:, :], in_=ps[:, 7, :], func=AF.Sqrt, bias=epsc[:, 0:1], scale=1.0 / 256.0)

                        def fin(e, opar=opar):
                            e.reciprocal(out=orst[:, :], in_=orst[:, :])
                            r = None
                            for ec in range(2):
                                e.tensor_scalar(out=of[:, ec, :], in0=of[:, ec, :], scalar1=gsub_fn(ec), scalar2=(1.0 - li),
                                                op0=ALU.mult, op1=ALU.mult)
                                r = e.tensor_tensor(out=oout[:, opar, ec, :], in0=of[:, ec, :], in1=orst[:, :], op=ALU.mult)
                            return r

                        sc.at(s, "dve", cmb)
                        sc.at(s + 1, "act", sq2)
                        sc.at(s + 2, "pe", mm2)
                        sc.at(s + 3, "act", rs)
                        sc.at(s + 4, "dve", fin)
                        s += 5
                    else:
                        def fin(e, opar=opar):
                            return e.tensor_copy(out=oout[:, opar, 0, :], in_=ob[:, 0, 0, :])

                        sc.at(s, "dve", fin)
                        s += 1

                    def st_o(e, hh=hh, q0=q0, opar=opar):
                        r = []
                        for ec in range(EC):
                            om, oo = orow_fn(hh, ec)
                            r.append(e.dma_start(out=drow(ou[:, q0:q0 + 512], om, oo, 128), in_=oout[:, opar, ec, :]))
                        return r

                    sc.at(s, "sp", st_o, ndma=EC)
                    s += 1
            sc.at(s, "pool", lambda e: [e.dma_start(out=f_(), in_=s_) for (f_, s_) in cout], ndma=len(cout))
            s += 1
            sc.region(rb, s, nunits)
            return sc.next()

        epsc = sb("epsc", [128, 1], F32)
        sc.at(1, "dve", lambda e: e.memset(epsc[:, :], EPS))

        def chk(k):
            if cfg.stop == k:
                raise StopBuild()

        try:
            s = sc.next()
            chk(0)
            for L in range(cfg.depth):
                j = L // 2
                gm = lambda c, L=L: gsb[:, (0 * cfg.depth + L) * DC + c:(0 * cfg.depth + L) * DC + c + 1]
                gl = lambda c, L=L: gsb[:, (1 * cfg.depth + L) * DC + c:(1 * cfg.depth + L) * DC + c + 1]
                gp = lambda c, L=L: gsb[:, (2 * cfg.depth + L) * DC + c:(2 * cfg.depth + L) * DC + c + 1]
                if L % 2 == 0:
                    li = 0.8 - 0.6 * math.exp(-0.3 * L)
                    rb = s
                    tok0 = TOK0
                    sc.at(s, "sp", lambda e: [e.dma_start(out=Xblk[:, :], in_=dcol(outT, TB, 0, TB)),
                                              e.dma_start(out=tabblk[:, :], in_=dcol(tabd, TB, 0, TB))], ndma=2)
                    s += 1
                    s = norm_phase(s, Xblk, DC, tok0, gm, hT)
                    chk(1)
                    if cfg.stop == 10:
                        s = gemm_fm(s, d_wqk, j * 4 * HD, 2 * HD, DC, hT, TB, epi_store_f32(latd, lambda jj: 0, tok0))
                        chk(10)
                    s = gemm_fm(s, d_wqk, j * 4 * HD, 2 * HD, DC, hT, TB,
                                epi_qk(qblk, lambda jj: jj * 128, tok0, lambda jj, j=j: dg[:, j * 8 + 0:j * 8 + 1], ONES, PERMD, 0))
                    s = gemm_fm(s, d_wqk, j * 4 * HD + 2 * HD, 2 * HD, DC, hT, TB,
                                epi_qk(kblk, lambda jj: jj * 128, tok0, lambda jj, j=j: dg[:, j * 8 + 1:j * 8 + 2], ONES, PERMD, 0))
                    chk(2)
                    s = gemm_tm(s, d_wv, j * 2 * HD, 2 * HD, DC, hT, tok0, vblk, 0)
                    chk(3)
                    nqr = 2 * HD * 128
                    sc.at(s, "pool", lambda e, nqr=nqr: [e.dma_start(out=dcol(qTd[0:nqr, :], TB, 0, TB), in_=qblk[0:nqr, :]),
                                                       e.dma_start(out=dcol(kTd[0:nqr, :], TB, 0, TB), in_=kblk[0:nqr, :]),
                                                       e.dma_start(out=drow(vd, TB, 0, TB), in_=vblk[:, :])], ndma=3)
                    s += 1
                    sc.region(rb, s, NTB)
                    s = attention(s, HD, 1, 2, 256,
                                  lambda hh, m: [(0, m * 128, 128, 0)],
                                  lambda hh, m: [(0, m * 128, 128, 0)],
                                  lambda hh: (0, 0),
                                  lambda hh, ec: (0, ec * 128),
                                  128 ** -0.5,
                                  [(qu[0:256, :], lambda: drow(qTd, 256, 0, 256)),
                                   (ku[0:256, :], lambda: drow(kTd, 256, 0, 256)),
                                   (vu[:, 0:256], lambda: dcol(vd, 256, 0, 256))],
                                  [(lambda: drow(oTd, 256, 0, 256), ou[0:256, :])],
                                  diff_j=j,
                                  gsub_fn=lambda ec, j=j: dg[:, j * 8 + 2 + ec:j * 8 + 3 + ec], li=li)
                    chk(4)
                    wo, wob, okc = d_wo, j * DC, 2 * HD
                else:
                    gb = j * (QC + KVC + 4)
                    rb = s
                    tok0 = TOK0
                    sc.at(s, "sp", lambda e: [e.dma_start(out=Xblk[:, :], in_=dcol(outT, TB, 0, TB)),
                                              e.dma_start(out=tabblk[:, :], in_=dcol(tabd, TB, 0, TB))], ndma=2)
                    s += 1
                    s = norm_phase(s, Xblk, DC, tok0, gm, hT)
                    nw = QC + KVC + 1
                    s = gemm_fm(s, m_win, j * nw, QC + KVC, DC, hT, TB, epi_store_f32(latd, lambda jj: jj * 128, tok0))
                    s = gemm_fm(s, m_win, j * nw + QC + KVC, 1, DC, hT, TB,
                                epi_qk(kblk, lambda jj: HM * 128, tok0, lambda jj, gb=gb: mg[:, gb + QC + KVC + 3:gb + QC + KVC + 4],
                                       ONES64, PERMM, 1))
                    cq = aT[:, 0, :, :]
                    s = norm_phase(s, latd, QC, tok0, lambda c, gb=gb: mg[:, gb + c:gb + c + 1], cq)
                    s = gemm_fm(s, m_wuq, j * (HM + HM // 2), HM, QC, cq, TB,
                                epi_qk(qblk, lambda jj: jj * 128, tok0, lambda jj, gb=gb: mg[:, gb + QC + KVC:gb + QC + KVC + 1], ONES, None, 1))
                    s = gemm_fm(s, m_wuq, j * (HM + HM // 2) + HM, HM // 2, QC, cq, TB,
                                epi_qk(qblk, lambda jj: (HM + jj) * 128, tok0,
                                       lambda jj, gb=gb: mg[:, gb + QC + KVC + 1:gb + QC + KVC + 2], ONES64, PERMM, 1))
                    ckv = aT[:, 1, :, :]
                    s = norm_phase(s, latd[QC * 128:(QC + KVC) * 128, :], KVC, tok0,
                                   lambda c, gb=gb: mg[:, gb + QC + c:gb + QC + c + 1], ckv)
                    s = gemm_fm(s, m_wuk, j * HM, HM, KVC, ckv, TB,
                                epi_qk(kblk, lambda jj: jj * 128, tok0, lambda jj, gb=gb: mg[:, gb + QC + KVC + 2:gb + QC + KVC + 3], ONES, None, 1))
                    s = gemm_tm(s, m_wuv, j * HM, HM, KVC, ckv, tok0, vblk, 0)
                    nqr = (HM + HM // 2) * 128
                    nkr = (HM + 1) * 128
                    sc.at(s, "pool", lambda e, nqr=nqr, nkr=nkr: [e.dma_start(out=dcol(qTd[0:nqr, :], TB, 0, TB), in_=qblk[0:nqr, :]),
                                                                e.dma_start(out=dcol(kTd[0:nkr, :], TB, 0, TB), in_=kblk[0:nkr, :]),
                                                                e.dma_start(out=drow(vd[:, 0:HM * 128], TB, 0, TB), in_=vblk[:, 0:HM * 128])], ndma=3)
                    s += 1
                    sc.region(rb, s, NTB)
                    s = attention(s, HM // 2, 2, 1, 128,
                                  lambda hh, m: [(0, hh * 128, 128, 0), (0, 256 + hh * 64, 64, hh * 64)],
                                  lambda hh, m: [(0, hh * 128, 128, 0), (0, 256 + hh * 64, 64, hh * 64)],
                                  lambda hh: (0, hh * 128),
                                  lambda hh, ec: (0, hh * 128),
                                  192 ** -0.5,
                                  [(qu[0:256, :], lambda: drow(qTd, 256, 0, 256)),
                                   (qu[256:384, :], lambda: drow(qTd, 128, HM * 128, 128)),
                                   (ku[0:256, :], lambda: drow(kTd, 256, 0, 256)),
                                   (ku[256:384, :], lambda: kTd[HM * 128:HM * 128 + 128, :]),
                                   (vu[:, 0:256], lambda: dcol(vd, 256, 0, 256))],
                                  [(lambda: drow(oTd, 256, 0, 256), ou[0:256, :])])
                    wo, wob, okc = m_wo, j * DC, HM
                rb = s
                tok0 = TOK0
                sc.at(s, "sp", lambda e, okc=okc, L=L: [e.dma_start(out=Xblk[:, :], in_=dcol(outT, TB, 0, TB)),
                                                        e.dma_start(out=oblk[0:okc * 128, :], in_=dcol(oTd[0:okc * 128, :], TB, 0, TB)),
                                                        e.dma_start(out=pblk[:, :], in_=dcol(pT[L * cfg.ple:(L + 1) * cfg.ple, :], TB, 0, TB))], ndma=3)
                s += 1
                def ldo(e, tok0=tok0, okc=okc):
                    return [e.dma_start(out=hT[:, 0:okc, :], in_=dcol(oblk[0:okc * 128, :], TBM, 0, TB).rearrange("(c p) t -> p c t", p=128))]

                sc.at(s, "sp", ldo, ndma=1)
                s += 1
                s = gemm_fm(s, wo, wob, DC, okc, hT, TB, epi_accum(tok0), nh_per_task=2)
                chk(5)
                s = norm_phase(s, Xblk, DC, tok0, gl, hT)
                for hb in range(NHB):
                    aset = hb % 2
                    s = gemm_fm(s, w1, L * FC + hb * HB, HB, DC, hT, TB, epi_relu2(aset))
                    s = gemm_fm(s, w2, (L * NHB + hb) * DC, DC, HB, aT[:, aset, :, :], TB, epi_accum(tok0), nh_per_task=2)
                chk(6)
                s = norm_phase(s, Xblk, DC, tok0, gp, hT)

                def ldp(e, tok0=tok0, L=L):
                    return [e.dma_start(out=pTs[:, :, :], in_=dcol(pblk[:, :], TBM, 0, TB).rearrange("(c p) t -> p c t", p=128))]

                sc.at(s, "pool", ldp, ndma=1)
                s += 1
                s = gemm_fm(s, wg, L * DC, DC, DC, hT, TB, epi_gate(tok0), extra_kc=PC, extra_in=pTs)
                chk(7)
                sc.at(s, "pool", lambda e: [e.dma_start(out=dcol(outT, TB, 0, TB), in_=Xblk[:, :])], ndma=1)
                s += 1
                sc.region(rb, s, NTB)

        except StopBuild:
            pass
        sc.emit(nc, block, sems)
    return nc


def prep_shared(cfg, inp):
    D, DC, depth = cfg.D, cfg.DC, cfg.depth
    FC = cfg.DFF // 128
    HB = cfg.HB
    NHB = FC // HB
    HD, HM = cfg.diff_heads, cfg.mla_heads
    QC, KVC = cfg.q_rank // 128, cfg.kv_rank // 128
    ND, NM = (depth + 1) // 2, depth // 2
    f = lambda a: np.asarray(a, dtype=np.float32)
    out = {}
    ct = const_tables(cfg)
    out["consts"] = np.concatenate([ct["ones"], ct["ones64"], ct["perm_d"], ct["perm_m"], ct["cols"]], axis=1)
    out["gcols"] = np.concatenate([col_layout(f(inp[k])[L], DC) for k in ("g_mix", "g_mlp", "g_ple") for L in range(depth)], axis=1)
    seq = lambda n: np.arange(n).reshape(-1, 128)
    out["w1"] = np.concatenate([tile_w(f(inp["w1"])[L], seq(cfg.DFF)) for L in range(depth)], 0)
    w2l = []
    for L in range(depth):
        W = f(inp["w2"])[L]
        for hb in range(NHB):
            w2l.append(tile_w(W[hb * HB * 128:(hb + 1) * HB * 128, :], seq(D)))
    out["w2"] = np.concatenate(w2l, 0)
    out["wg"] = np.concatenate([np.concatenate([tile_w(f(inp["w_gate"])[L], seq(D)), tile_w(f(inp["w_ple"])[L], seq(D))], axis=2)
                                for L in range(depth)], 0)
    QKW = HD * 256
    out["d_wqk"] = np.concatenate([tile_w(f(inp["diff_w_in"])[j][:, :2 * QKW], seq(2 * QKW)) for j in range(ND)], 0)
    out["d_wv"] = np.concatenate([tile_w_tm(f(inp["diff_w_in"])[j][:, 2 * QKW:]) for j in range(ND)], 0)
    out["d_wo"] = np.concatenate([tile_w(f(inp["diff_w_out"])[j], seq(D)) for j in range(ND)], 0)
    dgl = []
    for j in range(ND):
        lam = f(inp["diff_lambda"])[j]
        gs = f(inp["diff_g_sub"])[j]
        dgl.append(np.stack([f(inp["diff_g_q"])[j], f(inp["diff_g_k"])[j], gs[:128], gs[128:], lam[0], lam[1], lam[2], lam[3]], axis=1))
    out["d_g"] = np.ascontiguousarray(np.concatenate(dgl, 1))
    win_cols = np.concatenate([seq(cfg.q_rank + cfg.kv_rank),
                               (cfg.q_rank + cfg.kv_rank + np.tile(np.arange(64), 2)).reshape(1, 128)], 0)
    out["m_win"] = np.concatenate([tile_w(f(inp["mla_w_in"])[j], win_cols) for j in range(NM)], 0)
    nope_cols = np.stack([h * 192 + np.arange(128) for h in range(HM)])
    rope_cols = np.stack([np.concatenate([(2 * i) * 192 + 128 + np.arange(64), (2 * i + 1) * 192 + 128 + np.arange(64)]) for i in range(HM // 2)])
    out["m_wuq"] = np.concatenate([tile_w(f(inp["mla_w_uq"])[j], np.concatenate([nope_cols, rope_cols], 0)) for j in range(NM)], 0)
    kn_cols = np.stack([h * 256 + np.arange(128) for h in range(HM)])
    out["m_wuk"] = np.concatenate([tile_w(f(inp["mla_w_ukv"])[j], kn_cols) for j in range(NM)], 0)
    v_cols = np.concatenate([h * 256 + 128 + np.arange(128) for h in range(HM)])
    out["m_wuv"] = np.concatenate([tile_w_tm(f(inp["mla_w_ukv"])[j][:, v_cols]) for j in range(NM)], 0)
    out["m_wo"] = np.concatenate([tile_w(f(inp["mla_w_out"])[j], seq(D)) for j in range(NM)], 0)
    mgl = []
    for j in range(NM):
        gq, gk = f(inp["mla_g_q"])[j], f(inp["mla_g_k"])[j]
        mgl.append(np.concatenate([col_layout(f(inp["mla_g_cq"])[j], QC), col_layout(f(inp["mla_g_ckv"])[j], KVC),
                                   np.stack([gq[:128], np.tile(gq[128:], 2), gk[:128], np.tile(gk[128:], 2)], axis=1)], axis=1))
    out["m_g"] = np.ascontiguousarray(np.concatenate(mgl, 1))
    return out


def run(cfg, inp):
    nc = build(cfg)
    shared = prep_shared(cfg, inp)
    x = np.asarray(inp["x"], np.float32)
    p = np.asarray(inp["p"], np.float32)
    pos = np.asarray(inp["positions"]).astype(np.int32)
    B = x.shape[0]
    in_maps = []
    for b in range(B):
        m = dict(shared)
        m["xT"] = np.ascontiguousarray(x[b].T)
        m["pT"] = np.ascontiguousarray(np.concatenate([p[L, b].T for L in range(cfg.depth)], 0))
        m["posb"] = np.ascontiguousarray(np.broadcast_to(pos[b][None, :], (128, cfg.T)))
        in_maps.append(m)
    res = run_bass_kernel_spmd(nc, in_maps, core_ids=list(range(B)))
    return np.stack([np.ascontiguousarray(res.results[b]["outT"].T) for b in range(B)], 0).astype(np.float32)


def kernel(**inputs):
    return run(Cfg(), inputs)
```
